# Optimizing a Trainium2 kernel written in Bass

```python
import math
import jax, jax.numpy as jnp
from jax import lax
import numpy as np

D_MODEL = 1024
BATCH = 2
SEQ = 16384
DEPTH = 1

HG_HEADS = 4
HG_KEY = 128
HG_VAL = 128
HG_WIDTH = HG_HEADS * HG_VAL
HG_CHUNK = 64
ATT_HEADS = 8
ATT_HEAD_DIM = 64
ATT_WIDTH = ATT_HEADS * ATT_HEAD_DIM
DILATED_PATTERNS = ((128, 1), (512, 4), (2048, 16))
MIX_WIDTH = HG_WIDTH + ATT_WIDTH
NUM_BUCKETS = 32
MAX_DISTANCE = 1024
D_FF = -(-8 * D_MODEL // (3 * 256)) * 256
EPS = 1e-6
IN_SPLITS = (HG_WIDTH, HG_WIDTH, HG_WIDTH, HG_WIDTH, HG_WIDTH, ATT_WIDTH, ATT_WIDTH, ATT_WIDTH)
IN_WIDTH = sum(IN_SPLITS)

kernel_name = 'hybrid_hgrn2_dilated_attn_adaln_encoder'


def rms_norm(x, w):
    xf = x.astype(jnp.float32)
    y = xf * lax.rsqrt(jnp.mean(xf * xf, axis=-1, keepdims=True) + EPS)
    return (y * w.astype(jnp.float32)).astype(x.dtype)


def t5_bucket(rel):
    half = NUM_BUCKETS // 2
    max_exact = half // 2
    ret = jnp.where(rel > 0, half, 0)
    n = jnp.abs(rel)
    nf = jnp.maximum(n, 1).astype(jnp.float32)
    large = max_exact + (jnp.log(nf / max_exact) / math.log(MAX_DISTANCE / max_exact)
                         * (half - max_exact)).astype(jnp.int32)
    large = jnp.minimum(large, half - 1)
    return ret + jnp.where(n < max_exact, n, large)


def hgrn2_scan(q, k, v, log_f):
    B, S, H, N = q.shape
    V = v.shape[-1]
    C = HG_CHUNK
    nc = S // C

    def chunks(t):
        return jnp.moveaxis(t.astype(jnp.float32).reshape(B, nc, C, H, t.shape[-1]), 1, 0)

    lower = jnp.tril(jnp.ones((C, C), dtype=bool))

    def step(state, inp):
        qc, kc, vc, gc = inp
        G = jnp.cumsum(gc, axis=1)
        inter = jnp.einsum('bthn,bhnv->bthv', qc * jnp.exp(G), state)
        diff = G[:, :, None] - G[:, None, :]
        decay = jnp.exp(jnp.where(lower[None, :, :, None, None], diff, -jnp.inf))
        scores = jnp.einsum('bthn,bshn,btshn->bhts', qc, kc, decay)
        intra = jnp.einsum('bhts,bshv->bthv', scores, vc)
        G_last = G[:, -1]
        new_state = (jnp.exp(G_last)[..., None] * state
                     + jnp.einsum('bshn,bshv->bhnv', kc * jnp.exp(G_last[:, None] - G), vc))
        return new_state, inter + intra

    init = jnp.zeros((B, H, N, V), jnp.float32)
    _, out = lax.scan(step, init, (chunks(q), chunks(k), chunks(v), chunks(log_f)))
    return jnp.moveaxis(out, 0, 1).reshape(B, S, H, V)


def hgrn2_mixer(q, f_fwd, f_bwd, i, g, lb, gn_w):
    B, S, _ = q.shape
    qh = q.reshape(B, S, HG_HEADS, HG_KEY)
    vh = i.reshape(B, S, HG_HEADS, HG_VAL)

    def gates(z, lbd):
        z = z.astype(jnp.float32).reshape(B, S, HG_HEADS, HG_KEY)
        lbd = lbd.reshape(HG_HEADS, HG_KEY)
        log_f = jnp.logaddexp(jnp.log(lbd), jnp.log1p(-lbd) + jax.nn.log_sigmoid(z))
        k = (1.0 - lbd) * jax.nn.sigmoid(-z)
        return log_f, k

    logf_f, k_f = gates(f_fwd, lb[0])
    logf_b, k_b = gates(f_bwd, lb[1])
    o_f = hgrn2_scan(qh, k_f, vh, logf_f)
    flip = lambda t: jnp.flip(t, axis=1)
    o_b = flip(hgrn2_scan(flip(qh), flip(k_b), flip(vh), flip(logf_b)))
    o = (o_f + o_b)
    o = rms_norm(o, gn_w.reshape(HG_HEADS, HG_VAL)).reshape(B, S, HG_WIDTH)
    return o * jax.nn.silu(g.astype(jnp.float32))


def dilated_branch(q, k, v, rel_bias, window, dilation):
    B, S, H, Dh = q.shape
    R = window // (2 * dilation)
    M = S // dilation
    nb = -(-M // R)
    Mp = nb * R

    def to_blocks(t):
        t = t.reshape(B, M, dilation, H, Dh)
        t = jnp.pad(t, ((0, 0), (0, Mp - M), (0, 0), (0, 0), (0, 0)))
        return t.reshape(B, nb, R, dilation, H, Dh)

    def neighbours(t):
        tp = jnp.pad(t, ((0, 0), (1, 1), (0, 0), (0, 0), (0, 0), (0, 0)))
        return jnp.concatenate([tp[:, :-2], tp[:, 1:-1], tp[:, 2:]], axis=2)

    qb = to_blocks(q)
    kn = neighbours(to_blocks(k))
    vn = neighbours(to_blocks(v))

    off = (jnp.arange(3 * R)[None, :] - R) - jnp.arange(R)[:, None]
    band = jnp.abs(off) <= R
    kidx = jnp.arange(nb)[:, None] * R + jnp.arange(3 * R)[None, :] - R
    valid = (kidx >= 0) & (kidx < M)
    mask = band[None] & valid[:, None, :]
    bias = jnp.transpose(rel_bias[t5_bucket(off * dilation)], (2, 0, 1)).astype(jnp.float32)

    scale = ATT_HEAD_DIM ** -0.5
    logits = jnp.einsum('bnqrhd,bnkrhd->bnrhqk', qb, kn).astype(jnp.float32) * scale + bias
    logits = jnp.where(mask[None, :, None, None], logits, -jnp.inf)
    mx = jnp.max(logits, axis=-1, keepdims=True)
    p = jnp.exp(logits - mx)
    den = jnp.sum(p, axis=-1, keepdims=True)
    out = jnp.einsum('bnrhqk,bnkrhd->bnqrhd', p / den, vn.astype(jnp.float32))
    lse = (mx + jnp.log(den))[..., 0]
    out = out.reshape(B, Mp, dilation, H, Dh)[:, :M].reshape(B, S, H, Dh)
    lse = jnp.transpose(lse, (0, 1, 4, 2, 3)).reshape(B, Mp, dilation, H)[:, :M].reshape(B, S, H)
    return out, lse


def dilated_attention(q, k, v, rel_bias):
    B, S, _ = q.shape
    qh = q.reshape(B, S, ATT_HEADS, ATT_HEAD_DIM)
    kh = k.reshape(B, S, ATT_HEADS, ATT_HEAD_DIM)
    vh = v.reshape(B, S, ATT_HEADS, ATT_HEAD_DIM)
    outs, lses = [], []
    for window, dilation in DILATED_PATTERNS:
        o, l = dilated_branch(qh, kh, vh, rel_bias, window, dilation)
        outs.append(o)
        lses.append(l)
    w = jax.nn.softmax(jnp.stack(lses, axis=0), axis=0)
    o = jnp.einsum('pbsh,pbshd->bshd', w, jnp.stack(outs, axis=0))
    return o.reshape(B, S, ATT_WIDTH)


def setup_inputs(seed: int = 0) -> dict:
    key = jax.random.key(seed)
    ks = jax.random.split(key, 20)
    f32 = jnp.float32
    nrm = lambda k, shape, s: (jax.random.normal(k, shape, f32) * s)
    gain = lambda k, shape: 1.0 + 0.05 * jax.random.normal(k, shape, f32)
    x = jax.random.normal(ks[0], (BATCH, SEQ, D_MODEL), f32)
    c = jax.random.normal(ks[1], (BATCH, D_MODEL), f32)
    rel_bias = nrm(ks[2], (NUM_BUCKETS, ATT_HEADS), 0.5)
    w_ada = nrm(ks[3], (DEPTH, D_MODEL, 6 * D_MODEL), 0.5 * D_MODEL ** -0.5)
    b_ada = nrm(ks[4], (DEPTH, 6 * D_MODEL), 0.02)
    norm1_w = gain(ks[5], (DEPTH, D_MODEL))
    w_in = nrm(ks[6], (DEPTH, D_MODEL, IN_WIDTH), D_MODEL ** -0.5)
    hg_lower_bound = nrm(ks[7], (DEPTH + 1, 2, HG_WIDTH), 0.5)
    hg_norm_w = gain(ks[8], (DEPTH, HG_WIDTH))
    attn_norm_w = gain(ks[9], (DEPTH, ATT_WIDTH))
    w_out = nrm(ks[10], (DEPTH, MIX_WIDTH, D_MODEL), MIX_WIDTH ** -0.5)
    norm2_w = gain(ks[11], (DEPTH, D_MODEL))
    w_gate = nrm(ks[12], (DEPTH, D_MODEL, D_FF), D_MODEL ** -0.5)
    w_up = nrm(ks[13], (DEPTH, D_MODEL, D_FF), D_MODEL ** -0.5)
    w_down = nrm(ks[14], (DEPTH, D_FF, D_MODEL), D_FF ** -0.5)
    final_norm_w = gain(ks[15], (D_MODEL,))
    return {'x': x, 'c': c, 'rel_bias': rel_bias, 'w_ada': w_ada, 'b_ada': b_ada,
            'norm1_w': norm1_w, 'w_in': w_in, 'hg_lower_bound': hg_lower_bound,
            'hg_norm_w': hg_norm_w, 'attn_norm_w': attn_norm_w, 'w_out': w_out,
            'norm2_w': norm2_w, 'w_gate': w_gate, 'w_up': w_up, 'w_down': w_down,
            'final_norm_w': final_norm_w}


def reference(x, c, rel_bias, w_ada, b_ada, norm1_w, w_in, hg_lower_bound, hg_norm_w,
              attn_norm_w, w_out, norm2_w, w_gate, w_up, w_down, final_norm_w):
    lb_all = jnp.cumsum(jax.nn.softmax(hg_lower_bound.astype(jnp.float32), axis=0), axis=0)
    split_at = [int(v) for v in np.cumsum(IN_SPLITS)[:-1]]
    for l in range(DEPTH):
        mod = jax.nn.silu(c) @ w_ada[l] + b_ada[l]
        shift1, scale1, gate1, shift2, scale2, gate2 = jnp.split(mod[:, None, :], 6, axis=-1)

        h = rms_norm(x, norm1_w[l]) * (1.0 + scale1) + shift1
        z = h @ w_in[l]
        hq, hf_f, hf_b, hi, hg, aq, ak, av = jnp.split(z, split_at, axis=-1)
        y_hg = hgrn2_mixer(hq, hf_f, hf_b, hi, hg, lb_all[l], hg_norm_w[l])
        y_att = rms_norm(dilated_attention(aq, ak, av, rel_bias), attn_norm_w[l])
        mix = jnp.concatenate([y_hg.astype(x.dtype), y_att.astype(x.dtype)], axis=-1) @ w_out[l]
        x = x + gate1 * mix

        h2 = rms_norm(x, norm2_w[l]) * (1.0 + scale2) + shift2
        ffn = (jax.nn.silu(h2 @ w_gate[l]) * (h2 @ w_up[l])) @ w_down[l]
        x = x + gate2 * ffn
    return rms_norm(x, final_norm_w)
```

```python
import numpy as np
import ml_dtypes
import concourse.bass as bass
import concourse.mybir as mybir
from concourse.bass_utils import run_bass_kernel_spmd

F32 = mybir.dt.float32
BF16 = mybir.dt.bfloat16
ALU = mybir.AluOpType
AF = mybir.ActivationFunctionType

S = 16384; D = 1024; WL = 6144; HALO = 1024; OWN = 4096; NCORE = 8
DFF = 2816; NFF = 22
EPS = 1e-6
PATTERNS = (1, 4, 16)
QH = 2048
TT2 = 256
OPT = {}
PV = 'dve'


class Slot:
    __slots__ = ('w', 'r', 'frozen')

    def __init__(self):
        self.w = None; self.r = {}; self.frozen = False


class Prog:
    ENG = ('sp', 'act', 'pe', 'dve', 'pool')
    NDMASEM = 8

    def __init__(self, nc):
        self.nc = nc
        self.streams = {e: [] for e in self.ENG}
        self.cnt = {e: 0 for e in self.ENG}
        self.seen = {e: {} for e in self.ENG}
        self.last = {e: None for e in self.ENG}
        self.sem = {}; self.dma_sems = {}
        self.dma_n = {e: 0 for e in self.ENG}
        self.dma_tok = {e: [] for e in self.ENG}
        self.open_dma = []
        self._ctx = []
        self.nins = 0
        for e in ('act', 'pe', 'dve', 'pool'):
            cm = nc.semaphore("s_" + e); self.sem[e] = cm.__enter__(); self._ctx.append(cm)
        for e in ('sp', 'act', 'pool'):
            l = []
            for i in range(self.NDMASEM):
                cm = nc.semaphore(f"d_{e}{i}"); l.append(cm.__enter__()); self._ctx.append(cm)
            self.dma_sems[e] = l

    def _waits(self, eng, deps):
        best = {}
        for d in deps:
            if d is None:
                continue
            sem, val, src = d
            if src == eng and eng == 'pe':
                continue
            k = id(sem)
            if self.seen[eng].get(k, 0) < val and best.get(k, (None, 0))[1] < val:
                best[k] = (sem, val)
        w = []
        for k, (sem, val) in best.items():
            self.seen[eng][k] = val
            w.append((sem, val))
        return w

    def _deps(self, reads, writes, extra):
        deps = list(extra)
        for s in reads:
            if s.w is not None:
                deps.append(s.w)
        for s in writes:
            deps.extend(s.r.values())
            if s.w is not None:
                deps.append(s.w)
        return deps

    def _update(self, tok, reads, writes):
        for s in reads:
            if not s.frozen:
                k = id(tok[0])
                if k not in s.r or s.r[k][1] < tok[1]:
                    s.r[k] = tok
        for s in writes:
            s.w = tok; s.r = {}

    def op(self, eng, fn, reads=(), writes=(), extra=()):
        w = self._waits(eng, self._deps(reads, writes, extra))
        self.cnt[eng] += 1
        tok = (self.sem[eng], self.cnt[eng], eng)
        sem = self.sem[eng]

        def run(e, fn=fn, w=w, sem=sem):
            for (s, v) in w:
                e.wait_ge(s, v)
            fn(e).then_inc(sem, 1)
        self.streams[eng].append(run)
        self.nins += 1 + len(w)
        self.last[eng] = tok
        self._update(tok, reads, writes)
        return tok

    def dma(self, eng, out, in_, reads=(), writes=(), extra=()):
        j = self.dma_n[eng]; self.dma_n[eng] += 1
        sem = self.dma_sems[eng][j % self.NDMASEM]
        val = 16 * (j // self.NDMASEM + 1)
        deps = self._deps(reads, writes, extra)
        if j >= self.NDMASEM:
            deps.append(self.dma_tok[eng][j - self.NDMASEM])
        w = self._waits(eng, deps)

        def run(e, w=w, sem=sem, out=out, in_=in_):
            for (s, v) in w:
                e.wait_ge(s, v)
            e.dma_start(out=out, in_=in_).then_inc(sem, 16)
        self.streams[eng].append(run)
        self.nins += 1 + len(w)
        tok = (sem, val, 'dma_' + eng)
        self.dma_tok[eng].append(tok)
        self.open_dma.append(tok)
        self._update(tok, reads, writes)
        return tok

    def wait(self, eng, deps):
        w = self._waits(eng, deps)

        def run(e, w=w):
            for (s, v) in w:
                e.wait_ge(s, v)
        self.streams[eng].append(run)

    def barrier(self):
        toks = [t for t in self.last.values() if t is not None] + self.open_dma
        self.open_dma = []
        for e in self.ENG:
            self.wait(e, toks)

    def finish(self):
        nc = self.nc
        with nc.Block() as block:
            @block.sync
            def _(e):
                for f in self.streams['sp']: f(e)

            @block.scalar
            def _(e):
                for f in self.streams['act']: f(e)

            @block.tensor
            def _(e):
                for f in self.streams['pe']: f(e)

            @block.vector
            def _(e):
                for f in self.streams['dve']: f(e)

            @block.gpsimd
            def _(e):
                for f in self.streams['pool']: f(e)
        for cm in reversed(self._ctx):
            cm.__exit__(None, None, None)


def _t5_bucket(rel):
    half = 16; max_exact = 8
    ret = np.where(rel > 0, half, 0)
    n = np.abs(rel)
    nf = np.maximum(n, 1).astype(np.float32)
    large = max_exact + (np.log(nf / max_exact) / np.float32(np.log(1024 / max_exact)) * (half - max_exact)).astype(np.int32)
    large = np.minimum(large, half - 1)
    return ret + np.where(n < max_exact, n, large)


def _delta_tile():
    kk = np.arange(128)[:, None]; qq = np.arange(128)[None, :]
    dA = kk - 64 - qq; dB = kk + 64 - qq
    return np.concatenate([dA, dA, dB, dB], axis=1)


def _att_geometry():
    tiles = {}
    for half in range(2):
        for d in PATTERNS:
            nq = QH // (128 * d)
            for r in range(d):
                for kc in range(nq + 1):
                    mk0 = (HALO + QH * half) // d - 64 + 128 * kc
                    tiles[(half, d, r, kc)] = (mk0 * d + r, len(tiles))
    return tiles


ATT_TILES = _att_geometry()
NKV = len(ATT_TILES)


def _host_consts():
    c = {}
    c['ident'] = np.eye(128, dtype=np.float32).astype(ml_dtypes.bfloat16)
    p = np.arange(128)[:, None] % 64; t = np.arange(256)[None, :] % 64
    hm = np.stack([(p <= t), (p >= t)], axis=1).astype(np.float32)
    c['hmask'] = hm.astype(ml_dtypes.bfloat16)
    pp = np.arange(128)[:, None]; cc = np.arange(512)[None, :]
    rowok = (pp // 64) == ((cc // 64) % 2)
    sidx = pp % 64; tidx = cc % 64
    hm2 = np.stack([rowok & (sidx <= tidx), rowok & (sidx >= tidx)], axis=1).astype(np.float32)
    c['hmask2'] = hm2.astype(ml_dtypes.bfloat16)
    c['pmask'] = np.stack([(np.arange(128) < 64), (np.arange(128) >= 64)], axis=1).astype(np.float32)
    rm = np.ones((128, 512), np.float32); rm[:, 0::64] = 0.0
    c['resetm'] = rm
    dl = _delta_tile()
    c['band'] = (np.abs(dl) <= 64).astype(np.float32).astype(ml_dtypes.bfloat16)
    return c


def _bias_tables(rel_bias):
    dl = _delta_tile()
    out = np.zeros((3, 4, 128, 512), np.float32)
    hd = np.repeat(np.array([0, 1, 0, 1]), 128)[None, :]
    for pi, d in enumerate(PATTERNS):
        bk = _t5_bucket(dl * d)
        for pair in range(4):
            out[pi, pair] = rel_bias[bk, 2 * pair + hd]
    return out


def build_program(debug=False, stages=('x', 'hg', 'att', 'p2')):
    nc = bass.Bass("TRN2", target_bir_lowering=False)

    def din(name, shape, dt=F32):
        return nc.dram_tensor(name, list(shape), dt, kind="ExternalInput").ap()

    xw = din("xw", [D, WL]); cT = din("cT", [128, 8])
    w_ada = din("w_ada", [D, 6 * D]); b_ada = din("b_adaT", [128, 48])
    n1w = din("n1w", [128, 8]); w_in = din("w_in", [D, 4096])
    lbA = din("lbA", [128, 16])
    gnw = din("gnw", [128, 4]); anw = din("anw", [128, 4])
    w_out = din("w_out", [D, D]); n2w = din("n2w", [128, 8])
    w_gate = din("w_gate", [D, DFF]); w_up = din("w_up", [D, DFF]); w_down = din("w_down", [DFF, D])
    fnw = din("fnw", [128, 8])
    ident_d = din("ident", [128, 128], BF16); hmask_d = din("hmask", [128, 2, 256], BF16)
    hmask2_d = din("hmask2", [128, 2, 512], BF16); pmask_d = din("pmask", [128, 2])
    resetm_d = din("resetm", [128, 512]); band_d = din("band", [128, 512], BF16)
    biasT = din("biasT", [3, 4, 128, 512]); kvalid_d = din("kvalid", [128, NKV]); vvalid_d = din("vvalid", [128, 48])
    outT = nc.dram_tensor("outT", [D, OWN], F32, kind="ExternalOutput").ap()
    yT = nc.dram_tensor("yT", [D, OWN], BF16).ap()
    ssqd = nc.dram_tensor("ssqd", [8, QH], F32).ap()
    dbg = {}

    def dout(name, shape, dt=F32):
        dbg[name] = nc.dram_tensor(name, list(shape), dt, kind="ExternalOutput").ap()
        return dbg[name]

    ctxs = []

    def sb(name, shape, dt=F32):
        cm = nc.sbuf_tensor(name, list(shape), dt); t = cm.__enter__(); ctxs.append(cm); return t

    def psb(name, shape, dt=F32):
        cm = nc.psum_tensor(name, list(shape), dt); t = cm.__enter__(); ctxs.append(cm); return t

    P = Prog(nc)
    hn = sb("hn", [128, 8, WL], BF16)
    ident = sb("identb", [128, 128], BF16); ones = sb("onesb", [128, 128], BF16)
    hmask2 = sb("hmask2b", [128, 2, 512], BF16)
    resetm = sb("resetmb", [128, 512]); band = sb("bandb", [128, 512], BF16)
    vvalid = sb("vvalidb", [128, 48]); kvalid = sb("kvalidb", [128, NKV])
    small = sb("small", [128, 288])
    WP = sb("WP", [128, 8, 704], BF16)
    wst = sb("wst", [128, 2, 8, 128])
    ARENA_BYTES = 79872
    arena = sb("arena", [128, ARENA_BYTES // 2], BF16)
    pbank = [psb(f"pb{i}", [128, 512]) for i in range(8)]
    pslot = [Slot() for _ in range(8)]

    class Carver:
        def __init__(self):
            self.off = 0

        def get(self, shape, dt):
            esz = 4 if dt == F32 else 2
            n = int(np.prod(shape[1:])) * esz
            a = self.off // 2; b = (self.off + n) // 2
            self.off += n
            assert self.off <= ARENA_BYTES, f"arena overflow {self.off}"
            v = arena[:, a:b]
            if dt == F32:
                v = v.bitcast(F32)
            if len(shape) == 3:
                v = v.rearrange("p (a b) -> p a b", b=shape[2])
            return v

    col = {}
    _c = [0]

    def scol(name, n=1):
        col[name] = (_c[0], n); _c[0] += n; assert _c[0] <= 288
        return small[:, col[name][0]:col[name][0] + n]

    v_c = scol('c', 8); v_e = scol('ce', 8); v_sc = scol('sc', 8)
    v_mod = scol('mod', 48); v_bada = scol('bada', 48)
    v_n1w = scol('n1w', 8); v_g1 = scol('g1', 8); v_sg1 = scol('sg1', 8); v_rg1 = scol('rg1', 8)
    v_lbA = scol('lbA', 16); v_lb = scol('lb', 8); v_oml = scol('oml', 8); v_noml = scol('noml', 8)
    v_gnw = scol('gnw', 4); v_anw = scol('anw', 4)
    v_bias = scol('bias', 5); v_nbias = scol('nbias', 5)
    v_n2w = scol('n2w', 8); v_g2 = scol('g2', 8); v_sg2 = scol('sg2', 8); v_fnw = scol('fnw', 8)
    v_eps = scol('eps', 1); v_tmp = scol('tmp', 16)
    sl_small = Slot(); sl_const = Slot(); sl_hn = [Slot() for _ in range(12)]
    sl_WP = Slot(); sl_wst = [Slot(), Slot()]; sl_bias = Slot()

    def mm(out, lhsT, rhs, start, stop, reads, writes):
        return P.op('pe', lambda e: e.matmul(out, lhsT=lhsT, rhs=rhs, start=start, stop=stop, skip_group_check=True), reads, writes)

    def act(out, in_, func, reads, writes, scale=None, bias=None, eng='act'):
        kw = {}
        if scale is not None: kw['scale'] = scale
        if bias is not None: kw['bias'] = bias
        return P.op('act', lambda e: e.activation(out=out, in_=in_, func=func, **kw), reads, writes)

    def tt(eng, out, in0, in1, op, reads, writes):
        return P.op(eng, lambda e: e.tensor_tensor(out=out, in0=in0, in1=in1, op=op), reads, writes)

    def stt(eng, out, in0, scalar, in1, op0, op1, reads, writes):
        return P.op(eng, lambda e: e.scalar_tensor_tensor(out=out, in0=in0, scalar=scalar, in1=in1, op0=op0, op1=op1), reads, writes)

    def ts(eng, out, in0, s1, s2, op0, op1, reads, writes):
        if s2 is None:
            return P.op(eng, lambda e: e.tensor_scalar(out=out, in0=in0, scalar1=s1, scalar2=None, op0=op0), reads, writes)
        return P.op(eng, lambda e: e.tensor_scalar(out=out, in0=in0, scalar1=s1, scalar2=s2, op0=op0, op1=op1), reads, writes)

    def cp(eng, out, in_, reads, writes):
        if eng == 'act':
            return P.op('act', lambda e: e.activation(out=out, in_=in_, func=AF.Copy), reads, writes)
        return P.op(eng, lambda e: e.tensor_copy(out=out, in_=in_), reads, writes)

    def memset(eng, ap, val, writes):
        return P.op(eng, lambda e: e.memset(ap, val), (), writes)

    def sigmoid_parts(eng_v, out_r, in_ap, tmp_e, reads, writes_e, writes_r):
        act(tmp_e, in_ap, AF.Exp, reads, writes_e, scale=-1.0)
        ts(eng_v, tmp_e, tmp_e, 1.0, None, ALU.add, None, writes_e, writes_e)
        P.op(eng_v, lambda e: e.reciprocal(out=out_r, in_=tmp_e), writes_e, writes_r)

    P.dma('sp', ident[:], ident_d, (), [sl_const])
    P.dma('sp', hmask2[:], hmask2_d, (), [sl_const])
    P.dma('sp', resetm[:], resetm_d, (), [sl_const]); P.dma('sp', band[:], band_d, (), [sl_const])
    P.dma('sp', vvalid[:], vvalid_d, (), [sl_const]); P.dma('sp', kvalid[:], kvalid_d, (), [sl_const])
    P.dma('sp', v_c, cT, (), [sl_small]); P.dma('sp', v_bada, b_ada, (), [sl_small])
    P.dma('sp', v_n1w, n1w, (), [sl_small]); P.dma('sp', v_lbA, lbA, (), [sl_small])
    P.dma('sp', v_gnw, gnw, (), [sl_small]); P.dma('sp', v_anw, anw, (), [sl_small])
    P.dma('sp', v_n2w, n2w, (), [sl_small]); P.dma('sp', v_fnw, fnw, (), [sl_small])
    memset('pool', ones[:], 1.0, [sl_const]); memset('pool', v_eps, EPS, [sl_small])
    sigmoid_parts('dve', v_sc, v_c, v_e, [sl_small], [sl_small], [sl_small])
    tt('dve', v_sc, v_sc, v_c, ALU.mult, [sl_small], [sl_small])
    cv0 = Carver()
    wada_st = [cv0.get([128, 8, 512], F32), cv0.get([128, 8, 512], F32)]
    sl_wada = [Slot(), Slot()]
    xt = [cv0.get([128, 8, 512], F32), cv0.get([128, 8, 512], F32)]
    sq = cv0.get([128, 8, 512], BF16)
    rstd = cv0.get([128, 512], F32); lnv = cv0.get([128, 512], F32)
    sl_xt = [Slot(), Slot()]; sl_sq = Slot(); sl_sq2 = Slot(); sl_rstd = Slot(); sl_ln = Slot()
    xw_v = xw.rearrange("(c p) t -> p c t", p=128)
    w_ada_v = w_ada.rearrange("(kc p) n -> p kc n", p=128)

    def rstd_from_psum(ps_ap, ps_slot, nfeat, out_ap, out_slot, ln_ap, ln_slot):
        act(ln_ap, ps_ap, AF.Ln, [ps_slot, sl_small], [ln_slot], scale=1.0 / nfeat, bias=v_eps)
        act(out_ap, ln_ap, AF.Exp, [ln_slot], [out_slot], scale=-0.5)

    order = [2, 3, 0, 1] + list(range(4, 12))
    for ii, jt in enumerate(order):
        b = ii % 2
        P.dma('sp', wada_st[b], w_ada_v[:, :, jt * 512:(jt + 1) * 512], (), [sl_wada[b]])
        t = ii
        P.dma('sp', xt[b], xw_v[:, :, t * 512:(t + 1) * 512], (), [sl_xt[b]])
        for jj in range(4):
            j = jt * 4 + jj
            for kc in range(8):
                mm(pbank[7][:, j:j + 1], wada_st[b][:, kc, jj * 128:(jj + 1) * 128], v_sc[:, kc:kc + 1], kc == 0, kc == 7,
                   [sl_wada[b], sl_small], [pslot[7]])
        act(sq[:, 0:4, :], xt[b][:, 0:4, :], AF.Square, [sl_xt[b]], [sl_sq])
        tt('dve', sq[:, 4:8, :], xt[b][:, 4:8, :], xt[b][:, 4:8, :], ALU.mult, [sl_xt[b]], [sl_sq2])
        pb = t % 2
        for c in range(8):
            mm(pbank[pb][:, :], ones[:], sq[:, c, :], c == 0, c == 7, [sl_sq if c < 4 else sl_sq2, sl_const], [pslot[pb]])
        rstd_from_psum(pbank[pb][:, :], pslot[pb], D, rstd, sl_rstd, lnv, sl_ln)
        tt('dve', hn[:, :, t * 512:(t + 1) * 512], xt[b], rstd.unsqueeze(1).to_broadcast([128, 8, 512]), ALU.mult,
           [sl_xt[b], sl_rstd], [sl_hn[t]])
    for s_ in sl_hn:
        s_.frozen = True
    tt('dve', v_mod, pbank[7][:, 0:48], v_bada, ALU.add, [pslot[7], sl_small], [sl_small])
    m_sh1 = v_mod[:, 0:8]; m_sc1 = v_mod[:, 8:16]; m_g1 = v_mod[:, 16:24]
    m_sh2 = v_mod[:, 24:32]; m_sc2 = v_mod[:, 32:40]; m_g2 = v_mod[:, 40:48]
    stt('dve', v_g1, m_sc1, 1.0, v_n1w, ALU.add, ALU.mult, [sl_small], [sl_small])
    P.op('dve', lambda e: e.reciprocal(out=v_rg1, in_=v_g1), [sl_small], [sl_small])
    tt('dve', v_sg1, m_sh1, v_rg1, ALU.mult, [sl_small], [sl_small])
    stt('dve', v_g2, m_sc2, 1.0, v_n2w, ALU.add, ALU.mult, [sl_small], [sl_small])
    P.op('dve', lambda e: e.reciprocal(out=v_tmp[:, 0:8], in_=v_g2), [sl_small], [sl_small])
    tt('dve', v_sg2, m_sh2, v_tmp[:, 0:8], ALU.mult, [sl_small], [sl_small])
    lbv = v_lbA.rearrange("p (a l) -> p a l", l=2)
    tt('dve', v_tmp[:, 0:8], lbv[:, :, 1], lbv[:, :, 0], ALU.subtract, [sl_small], [sl_small])
    act(v_tmp[:, 8:16], v_tmp[:, 0:8], AF.Exp, [sl_small], [sl_small])
    ts('dve', v_tmp[:, 8:16], v_tmp[:, 8:16], 1.0, None, ALU.add, None, [sl_small], [sl_small])
    P.op('dve', lambda e: e.reciprocal(out=v_lb, in_=v_tmp[:, 8:16]), [sl_small], [sl_small])
    ts('dve', v_oml, v_lb, -1.0, 1.0, ALU.mult, ALU.add, [sl_small], [sl_small])
    ts('dve', v_noml, v_lb, 1.0, -1.0, ALU.mult, ALU.add, [sl_small], [sl_small])
    zt = Carver().get([128, 512], BF16); sl_zt = Slot()
    memset('dve', zt, 0.0, [sl_zt])
    mm(pbank[3][:, :], zt[:, 0:128], zt, True, True, [sl_zt], [pslot[3]]); mm(pbank[4][:, :], zt[:, 0:128], zt, True, True, [sl_zt], [pslot[4]])
    sg1b = sb("sg1b", [128, 8], BF16); sg2b = sb("sg2b", [128, 8], BF16)
    cp('dve', sg1b[:], v_sg1, [sl_small], [sl_small]); cp('dve', sg2b[:], v_sg2, [sl_small], [sl_small])
    if debug:
        P.dma("sp", dout("d_small", [128, 288]), small[:], [sl_small], ())
    P.barrier()

    if debug:
        P.dma('sp', dout("d_hn", [128, 8, WL], BF16), hn[:], sl_hn, ())
    P.barrier()

    w_in_v = w_in.rearrange("(kc p) n -> p kc n", p=128)
    wst_n = [0]

    def prep_group(col0s):
        for i, c0 in enumerate(col0s):
            b = wst_n[0] % 2; wst_n[0] += 1
            P.dma('sp', wst[:, b], w_in_v[:, :, c0:c0 + 128], (), [sl_wst[b]])
            tt('dve', WP[:, :, i * 128:(i + 1) * 128], wst[:, b], v_g1.unsqueeze(2).to_broadcast([128, 8, 128]), ALU.mult,
               [sl_wst[b], sl_small], [sl_WP])
        for i in range(len(col0s)):
            for kc in range(8):
                mm(pbank[7][:, 64 + i:65 + i], WP[:, kc, i * 128:(i + 1) * 128], sg1b[:, kc:kc + 1], kc == 0, kc == 7,
                   [sl_WP, sl_small], [pslot[7]])
        n = len(col0s)
        cp('dve', v_bias[:, 0:n], pbank[7][:, 64:64 + n], [pslot[7]], [sl_bias])
        ts('dve', v_nbias[:, 0:n], pbank[7][:, 64:64 + n], -1.0, None, ALU.mult, None, [pslot[7]], [sl_bias])

    ipn = [0]

    def inproj(i, t0, n, bank=None):
        if bank is None:
            pb = ipn[0] % 2; ipn[0] += 1
        else:
            pb = bank
        rs = [sl_WP] + [sl_hn[k] for k in range(t0 // 512, (t0 + n - 1) // 512 + 1)]
        for kc in range(8):
            mm(pbank[pb][:, 0:n], WP[:, kc, i * 128:(i + 1) * 128], hn[:, kc, t0:t0 + n], kc == 0, kc == 7, rs, [pslot[pb]])
        return pbank[pb][:, 0:n], pslot[pb]

    def hgrn_head(hh):
        prep_group([hh * 128, 512 + hh * 128, 1024 + hh * 128, 1536 + hh * 128, 2048 + hh * 128])
        cv = Carver()
        qT = cv.get([128, OWN], BF16); sgT = cv.get([128, OWN], BF16); Vtok = cv.get([128, 48, 128], BF16)
        oacc = cv.get([128, OWN], F32)
        sl_q = [Slot() for _ in range(8)]; sl_sg = [Slot() for _ in range(8)]; sl_V = [Slot() for _ in range(12)]
        sl_o = [Slot() for _ in range(8)]
        sets = []
        for _s in range(2):
            st = {}
            for n in ('X0', 'X1', 'X2', 'X3', 'X4'):
                st[n] = (cv.get([128, 512], F32), Slot())
            for n in ('A', 'B', 'EK', 'Qi'):
                st[n] = (cv.get([128, 512], BF16), Slot())
            st['KhT'] = (cv.get([128, 4, 128], BF16), Slot()); st['PT'] = (cv.get([128, 512], BF16), Slot())
            sets.append(st)
        b_vT, sl_vT = sets[0]['PT']
        Sst = [cv.get([128, 128], BF16) for _ in range(6)]
        psT = pbank[2].bitcast(BF16)
        for t in range(12):
            t0 = t * 512
            own = 2 <= t < 10
            st = sets[t % 2]
            if own:
                o0 = t0 - HALO; ot = t - 2
                (x0, s0), (x2, s2) = st['X0'], st['X2']
                ps, psl = inproj(0, t0, 512)
                ts('dve', qT[:, o0:o0 + 512], ps, v_bias[:, 0:1], None, ALU.add, None, [psl, sl_bias], [sl_q[ot]])
                ps, psl = inproj(4, t0, 512)
                act(x0, ps, AF.Exp, [psl, sl_bias], [s0], scale=-1.0, bias=v_nbias[:, 4:5])
                act(x2, x0, AF.Ln, [s0], [s2], bias=1.0)
                act(x2, x2, AF.Exp, [s2], [s2], scale=-1.0)
                stt('dve', sgT[:, o0:o0 + 512], ps, v_bias[:, 4:5], x2, ALU.add, ALU.mult, [psl, sl_bias, s2], [sl_sg[ot]])
            ps, psl = inproj(3, t0, 512)
            act(b_vT, ps, AF.Identity, [psl, sl_bias], [sl_vT], bias=v_bias[:, 3:4])
            for j in range(4):
                P.op('pe', lambda e, j=j: e.transpose(psT[:, j * 128:(j + 1) * 128], b_vT[:, j * 128:(j + 1) * 128], ident[:]),
                     [sl_vT, sl_const], [pslot[2]])
            tt('dve', Vtok[:, 4 * t:4 * t + 4, :], psT[:, 0:512].rearrange("p (a b) -> p a b", b=128),
               vvalid[:, 4 * t:4 * t + 4].unsqueeze(2).to_broadcast([128, 4, 128]), ALU.mult, [pslot[2], sl_const], [sl_V[t]])
        owr = set()
        Sdir = [[Sst[0], Sst[1], Sst[2]], [Sst[3], Sst[4], Sst[5]]]
        sl_Sd = [[Slot() for _ in range(3)] for _ in range(2)]
        sl_khp = [Slot(), Slot()]; sl_ds = [[Slot(), Slot()], [Slot(), Slot()]]
        scur_ = [0, 0]; dcnt = [0, 0]
        blocks_d = [list(range(0, 10)), list(range(11, 1, -1))]
        nsteps = OPT.get('hg_nblocks', 10)
        r3 = lambda a: a.rearrange("p (c j) -> p c j", j=64)
        psT1 = pbank[1].bitcast(BF16)
        for dr in range(2):
            memset('pool', Sdir[dr][0], 0.0, [sl_Sd[dr][0]])

        def stage_A1a(dr, t):
            fi = 1 + dr
            lbc = v_lb[:, 2 * hh + dr:2 * hh + dr + 1]; omlc = v_oml[:, 2 * hh + dr:2 * hh + dr + 1]; nomlc = v_noml[:, 2 * hh + dr:2 * hh + dr + 1]
            t0 = t * 512
            st = sets[dr]
            (X0, s0), (X1, s1), (X2, s2), (X3, s3), (X4, s4) = st['X0'], st['X1'], st['X2'], st['X3'], st['X4']
            ps, psl = inproj(fi, t0, 512, bank=0)
            act(X0, ps, AF.Exp, [psl, sl_bias], [s0], scale=-1.0, bias=v_nbias[:, fi:fi + 1])
            act(X2, X0, AF.Ln, [s0], [s2], bias=1.0)
            act(X2, X2, AF.Exp, [s2], [s2], scale=-1.0)
            act(X1, X2, AF.Ln, [s2, sl_small], [s1], scale=omlc, bias=lbc)
            act(X0, X2, AF.Identity, [s2, sl_small, s0], [s0], scale=nomlc, bias=omlc)
            P.op('dve', lambda e, X3=X3, X1=X1: e.tensor_tensor_scan(out=X3, data0=resetm[:], data1=X1, initial=0.0, op0=ALU.mult, op1=ALU.add),
                 [s1, sl_const], [s3])
            if dr == 0:
                tt('dve', r3(X2), r3(X3)[:, :, 63:64].to_broadcast([128, 8, 64]), r3(X3), ALU.subtract, [s3, s2], [s2])
                Gx, sG = X3, s3
                Dx, sD = X1, s1
            else:
                tt('dve', X2, X3, X1, ALU.subtract, [s3, s1, s2], [s2])
                tt('dve', r3(X1), r3(X3)[:, :, 63:64].to_broadcast([128, 8, 64]), r3(X2), ALU.subtract, [s3, s2, s1], [s1])
                Gx, sG = X1, s1
                Dx, sD = X3, s3
            tt('dve', r3(Dx), r3(Gx), r3(Gx)[:, :, 32:33].to_broadcast([128, 8, 64]), ALU.subtract, [sG, sD], [sD])
            ts('dve', Dx, Dx, 60.0, -60.0, ALU.min, ALU.max, [sD], [sD])
            return dict(dr=dr, t=t, t0=t0, own=(2 <= t < 10), Gx=Gx, sG=sG, Dx=Dx, sD=sD)

        def stage_A1b(cx):
            dr = cx['dr']; t = cx['t']; t0 = cx['t0']; own = cx['own']; Gx = cx['Gx']; sG = cx['sG']; Dx = cx['Dx']; sD = cx['sD']
            st = sets[dr]
            (X0, s0), (X1, s1), (X2, s2), (X4, s4) = st['X0'], st['X1'], st['X2'], st['X4']
            (bA, sA), (bB, sB), (bEK, sEK), (bQi, sQi) = st['A'], st['B'], st['EK'], st['Qi']
            act(bEK, X2, AF.Exp, [s2], [sEK])
            act(X4, Gx, AF.Exp, [sG, s4], [s4])
            tt('dve', bEK, X0, bEK, ALU.mult, [s0, sEK], [sEK])
            if own:
                o0 = t0 - HALO; ot = t - 2
                act(bA, Dx, AF.Exp, [sD], [sA])
                act(bB, Dx, AF.Exp, [sD], [sB], scale=-1.0)
                tt('pool', bA, qT[:, o0:o0 + 512], bA, ALU.mult, [sl_q[ot], sA], [sA])
                tt('dve', bQi, qT[:, o0:o0 + 512], X4, ALU.mult, [sl_q[ot], s4], [sQi])
                tt('pool', bB, X0, bB, ALU.mult, [s0, sB], [sB])

        def stage_A2(cx):
            dr = cx['dr']
            st = sets[dr]
            (bEK, sEK), (bKhT, sKhT) = st['EK'], st['KhT']
            for j in range(4):
                P.op('pe', lambda e, j=j, bEK=bEK: e.transpose(psT1[:, j * 128:(j + 1) * 128], bEK[:, j * 128:(j + 1) * 128], ident[:]),
                     [sEK, sl_const], [pslot[1]])
            cp('act', bKhT, psT1[:, 0:512].rearrange("p (a b) -> p a b", b=128), [pslot[1]], [sKhT])

        def stage_B2(cxl):
            info = []
            for cx in cxl:
                dr = cx['dr']; t = cx['t']; t0 = cx['t0']; own = cx['own']
                st = sets[dr]
                (bA, sA), (bB, sB), (bPT, sPT) = st['A'], st['B'], st['PT']
                corder = list(range(8)) if dr == 0 else list(range(7, -1, -1))
                sbank = 3 + dr
                if own:
                    for c in corder:
                        rows = slice((c % 2) * 64, (c % 2) * 64 + 64)
                        mm(pbank[sbank][rows, c * 64:(c + 1) * 64], bB[:, c * 64:(c + 1) * 64], bA[:, c * 64:(c + 1) * 64], True, True,
                           [sB, sA], [pslot[sbank]])
                    tt('dve', bPT, pbank[sbank][:, :], hmask2[:, dr, :], ALU.mult, [pslot[sbank], sl_const], [sPT])
                info.append(corder)
            for hf4 in range(2):
                for cx, corder in zip(cxl, info):
                    dr = cx['dr']; t = cx['t']; t0 = cx['t0']
                    st = sets[dr]; (bKhT, sKhT) = st['KhT']
                    dbank = 2 if dr == 0 else 7
                    for ci in range(4):
                        c = corder[hf4 * 4 + ci]
                        rows = slice((c % 2) * 64, (c % 2) * 64 + 64)
                        jt = (t0 + 64 * c) // 128
                        cc = ci * 128
                        if c % 2 == 0:
                            mm(pbank[dbank][:, cc:cc + 128], bKhT[rows, c // 2, :], Vtok[rows, jt, :], True, True, [sKhT, sl_V[t]], [pslot[dbank]])
                        else:
                            for hf in range(2):
                                mm(pbank[dbank][hf * 64:(hf + 1) * 64, cc:cc + 128], bKhT[rows, c // 2, hf * 64:(hf + 1) * 64], Vtok[rows, jt, :], True, True,
                                   [sKhT, sl_V[t]], [pslot[dbank]])
                for ci in range(4):
                    for cx, corder in zip(cxl, info):
                        dr = cx['dr']; t = cx['t']; t0 = cx['t0']; own = cx['own']
                        st = sets[dr]
                        tot_col = 63 if dr == 0 else 0
                        (X4, s4) = st['X4']; (bQi, sQi) = st['Qi']; (bPT, sPT) = st['PT']
                        dbank = 2 if dr == 0 else 7; obank = 5 + dr
                        c = corder[hf4 * 4 + ci]
                        jt = (t0 + 64 * c) // 128
                        cc = ci * 128
                        scur = scur_[dr]
                        if own:
                            mm(pbank[obank][:, c * 64:(c + 1) * 64], Vtok[:, jt, :], bPT[:, c * 64:(c + 1) * 64], True, False,
                               [sl_V[t], sPT], [pslot[obank]])
                            mm(pbank[obank][:, c * 64:(c + 1) * 64], Sdir[dr][scur], bQi[:, c * 64:(c + 1) * 64], False, True,
                               [sl_Sd[dr][scur], sQi], [pslot[obank]])
                        snx = (scur + 1) % 3
                        stt('dve', Sdir[dr][snx], Sdir[dr][scur], X4[:, c * 64 + tot_col:c * 64 + tot_col + 1], pbank[dbank][:, cc:cc + 128], ALU.mult, ALU.add,
                            [sl_Sd[dr][scur], s4, pslot[dbank]], [sl_Sd[dr][snx]])
                        scur_[dr] = snx
            for cx in cxl:
                dr = cx['dr']; t = cx['t']; t0 = cx['t0']; own = cx['own']
                if own:
                    o0 = t0 - HALO; ot = t - 2; obank = 5 + dr
                    if ot not in owr:
                        owr.add(ot)
                        cp('act', oacc[:, o0:o0 + 512], pbank[obank][:, :], [pslot[obank]], [sl_o[ot]])
                    else:
                        tt('dve', oacc[:, o0:o0 + 512], oacc[:, o0:o0 + 512], pbank[obank][:, :], ALU.add, [pslot[obank], sl_o[ot]], [sl_o[ot]])

        cxs = {}
        for dr in range(2):
            cxs[(dr, 0)] = stage_A1a(dr, blocks_d[dr][0])
        for dr in range(2):
            stage_A1b(cxs[(dr, 0)]); stage_A2(cxs[(dr, 0)])
        for i in range(nsteps):
            if i + 1 < nsteps:
                for dr in range(2):
                    cxs[(dr, i + 1)] = stage_A1a(dr, blocks_d[dr][i + 1])
            stage_B2([cxs[(0, i)], cxs[(1, i)]])
            if i + 1 < nsteps:
                for dr in range(2):
                    stage_A1b(cxs[(dr, i + 1)])
                for dr in range(2):
                    stage_A2(cxs[(dr, i + 1)])
        for ot in range(OPT.get('hg_fin', 8)):
            o0 = ot * 512
            st = sets[ot % 2]
            (X0, s0), (X1, s1), (X2, s2) = st['X0'], st['X1'], st['X2']
            (bA, sA), (bQi, sQi) = st['A'], st['Qi']
            act(bA, oacc[:, o0:o0 + 512], AF.Square, [sl_o[ot]], [sA])
            mm(pbank[3][:, :], ones[:], bA, True, True, [sA, sl_const], [pslot[3]])
            rstd_from_psum(pbank[3][:, :], pslot[3], 128, X0, s0, X1, s1)
            tt('dve', X2, oacc[:, o0:o0 + 512], X0, ALU.mult, [sl_o[ot], s0], [s2])
            stt('dve', bQi, X2, v_gnw[:, hh:hh + 1], sgT[:, o0:o0 + 512], ALU.mult, ALU.mult, [s2, sl_small, sl_sg[ot]], [sQi])
            P.dma('sp', yT[hh * 128:(hh + 1) * 128, o0:o0 + 512], bQi, [sQi], ())
        if debug and hh == 0:
            P.dma('sp', dout("d_ohg0", [128, OWN]), oacc, sl_o, ())
            P.dma('sp', dout("d_qT0", [128, OWN], BF16), qT, sl_q, ())
        P.barrier()

    if 'hg' in stages:
        for hh in range(OPT.get('hg_heads', 4)):
            hgrn_head(hh)

    hnb = hn[:].rearrange("p a b -> p (a b)")
    Wg = hnb[:, 0:8 * DFF].rearrange("p (a b) -> p a b", b=DFF)
    Wu = hnb[:, 8 * DFF:16 * DFF].rearrange("p (a b) -> p a b", b=DFF)
    sl_Wg = Slot(); sl_Wu = Slot()
    HW = DFF // 2
    wpf = WP[:].rearrange("p a b -> p (a b)")
    stg = [wpf[:, 0:2 * HW].bitcast(F32), wpf[:, 2 * HW:4 * HW].bitcast(F32), wst[:].rearrange("p a b c -> p (a b c)")[:, 0:HW]]
    sl_stg = [Slot(), Slot(), Slot()]
    stg_extra = [[sl_WP], [sl_WP], [sl_wst[0], sl_wst[1]]]
    stg_n = [0]
    wg_v = w_gate.rearrange("(kc p) n -> p kc n", p=128); wu_v = w_up.rearrange("(kc p) n -> p kc n", p=128)

    def load_cast(src_ap, ncol, dst_ap, scale_col, dst_slot, engs, extra=(), ring=None):
        if ring is None:
            k = stg_n[0] % 3; stg_n[0] += 1
            sbuf_, sslot_, sextra_ = stg[k], sl_stg[k], stg_extra[k]
        else:
            k = stg_n[0] % len(ring); stg_n[0] += 1
            sbuf_, sslot_, sextra_ = ring[k]
        P.dma('sp', sbuf_[:, 0:ncol], src_ap, (), [sslot_] + sextra_)
        eng = engs[stg_n[0] % len(engs)]
        rd = [sslot_, sl_small] + list(sextra_)
        stg_k = sbuf_
        if eng == 'act':
            P.op('act', lambda e: e.activation(out=dst_ap, in_=stg_k[:, 0:ncol], func=AF.Copy, scale=scale_col), rd, [dst_slot], extra)
        elif eng == 'dve':
            P.op('dve', lambda e: e.tensor_scalar(out=dst_ap, in0=stg_k[:, 0:ncol], scalar1=scale_col, scalar2=None, op0=ALU.mult), rd, [dst_slot], extra)
        else:
            P.op('pool', lambda e: e.tensor_tensor(out=dst_ap, in0=stg_k[:, 0:ncol], in1=scale_col.to_broadcast([128, ncol]), op=ALU.mult), rd, [dst_slot], extra)

    def prefetch_gate_weights():
        tok = P.last['pe']
        for kc in range(8):
            for h in range(2):
                load_cast(wg_v[:, kc, h * HW:(h + 1) * HW], HW, Wg[:, kc, h * HW:(h + 1) * HW], v_g2[:, kc:kc + 1], sl_Wg, ['pool'], extra=[tok])
        for kc in range(8):
            for h in range(2):
                load_cast(wu_v[:, kc, h * HW:(h + 1) * HW], HW, Wu[:, kc, h * HW:(h + 1) * HW], v_g2[:, kc:kc + 1], sl_Wu, ['pool'], extra=[tok])

    def att_pair(pair):
        prep_group([2560 + pair * 128, 3072 + pair * 128, 3584 + pair * 128])
        cv = Carver()
        KT = cv.get([128, WL], BF16); vT = cv.get([128, WL], BF16)
        QX = cv.get([128, 2, QH], BF16)
        VAB = cv.get([128, 2, 32, 128], BF16) if False else None
        VA = cv.get([128, 32, 128], BF16); VB = cv.get([128, 32, 128], BF16)
        accA = cv.get([128, QH], F32); accB = cv.get([128, QH], F32)
        E = [cv.get([128, 512], BF16) for _ in range(3)]
        bst = cv.get([128, 512], F32)
        expS = [cv.get([128, 512], BF16) for _ in range(3)]; PT = [cv.get([128, 512], BF16) for _ in range(3)]
        sqb = expS[0]
        Dn = VB[:].rearrange("p a b -> p (a b)").bitcast(F32)
        yb = VA[:].rearrange("p a b -> p (a b)")[:, 0:QH]
        sl_Q = Slot(); sl_K = [Slot() for _ in range(12)]; sl_v = [Slot() for _ in range(12)]
        sl_VA = Slot(); sl_VB = Slot(); sl_aA = Slot(); sl_aB = Slot(); sl_E = [Slot() for _ in range(3)]; sl_bst = Slot()
        sl_ex = [Slot(), Slot(), Slot()]; sl_PT = [Slot(), Slot(), Slot()]; sl_sqb = sl_ex[0]
        SB = [3, 4, 0, 1]
        psT = pbank[2].bitcast(BF16)
        for t in range(12):
            t0 = t * 512
            ps, psl = inproj(1, t0, 512)
            act(KT[:, t0:t0 + 512], ps, AF.Identity, [psl, sl_bias], [sl_K[t]], bias=v_bias[:, 1:2])
            ps, psl = inproj(2, t0, 512)
            ts('dve', vT[:, t0:t0 + 512], ps, v_bias[:, 2:3], None, ALU.add, None, [psl, sl_bias], [sl_v[t]])
        for pi in range(3):
            P.dma('sp', bst, biasT[pi, pair], (), [sl_bst])
            act(bst, bst, AF.Exp, [sl_bst], [sl_bst])
            tt('dve', E[pi], bst, band[:], ALU.mult, [sl_bst, sl_const], [sl_E[pi]])
        qn = [0]
        for half in range(OPT.get('att_halves', 2)):
            if OPT.get('att_stop', 99) <= 1: break
            memset('dve', QX[64:128, 0, :], 0.0, [sl_Q]); memset('dve', QX[0:64, 1, :], 0.0, [sl_Q])
            memset('dve', VA[:, :, 64:128], 1.0, [sl_VA]); memset('dve', VB[:, :, 0:64], 1.0, [sl_VB])
            for tq in range(QH // 512):
                t0 = HALO + half * QH + tq * 512
                ps, psl = inproj(0, t0, 512)
                ts('dve', QX[0:64, 0, tq * 512:(tq + 1) * 512], ps[0:64, :], v_bias[0:64, 0:1], 0.125, ALU.add, ALU.mult, [psl, sl_bias], [sl_Q])
                ts('dve', QX[64:128, 1, tq * 512:(tq + 1) * 512], ps[64:128, :], v_bias[64:128, 0:1], 0.125, ALU.add, ALU.mult, [psl, sl_bias], [sl_Q])
            if pair == OPT.get('att_pairs', 4) - 1 and half == 1 and 'p2' in stages:
                prefetch_gate_weights()
            for pi, d in enumerate(PATTERNS):
                if OPT.get('att_stop', 99) <= 2 or pi >= OPT.get('att_npat', 3): break
                if pi < OPT.get('att_pat0', 0): continue
                nq = QH // (128 * d)
                keys = [(r, kc) for r in range(d) for kc in range(nq + 1)]
                vidx = {}
                for g0 in range(0, len(keys), 4):
                    grp = keys[g0:g0 + 4]
                    for jj, (r, kc) in enumerate(grp):
                        w0, _ = ATT_TILES[(half, d, r, kc)]
                        src = vT[:, w0:w0 + 127 * d + 1:d] if d > 1 else vT[:, w0:w0 + 128]
                        tl = range(w0 // 512, (w0 + 127 * d) // 512 + 1)
                        P.op('pe', lambda e, jj=jj, src=src: e.transpose(psT[:, jj * 128:(jj + 1) * 128], src, ident[:]),
                             [sl_v[k] for k in tl] + [sl_const], [pslot[2]])
                        vidx[(r, kc)] = g0 + jj
                    n = len(grp)
                    pv3 = psT[:, 0:n * 128].rearrange("p (a b) -> p a b", b=128)
                    cp('act', VA[:, g0:g0 + n, 0:64], pv3[:, :, 0:64], [pslot[2]], [sl_VA])
                    cp('act', VB[:, g0:g0 + n, 64:128], pv3[:, :, 64:128], [pslot[2]], [sl_VB])
                qtiles = [(r, qt) for qt in range(nq) for r in range(d)]
                ntile = len(qtiles)
                info = {}
                if OPT.get('att_stop', 99) <= 3: continue

                def emit_qk(i):
                    r, qt = qtiles[i]
                    qb = qn[0] % 3; sbank = SB[qn[0] % 4]; qn[0] += 1
                    lq0 = 128 * d * qt + r
                    qsl = slice(lq0, lq0 + 127 * d + 1, d) if d > 1 else slice(lq0, lq0 + 128)
                    cols = []
                    for ch in range(2):
                        w0, kcol = ATT_TILES[(half, d, r, qt + ch)]
                        ksl = slice(w0, w0 + 127 * d + 1, d) if d > 1 else slice(w0, w0 + 128)
                        ktl = range(w0 // 512, (w0 + 127 * d) // 512 + 1)
                        mm(pbank[sbank][:, ch * 256:(ch + 1) * 256].rearrange("p (a b) -> p a b", b=128), KT[:, ksl], QX[:, :, qsl], True, True,
                           [sl_K[k] for k in ktl] + [sl_Q], [pslot[sbank]])
                        cols.append(kcol)
                    act(expS[qb], pbank[sbank][:, :], AF.Exp, [pslot[sbank]], [sl_ex[qb]])
                    interior = []
                    for ch in range(2):
                        w0, _ = ATT_TILES[(half, d, r, qt + ch)]
                        interior.append(w0 >= HALO and w0 + 127 * d < HALO + OWN)
                    if all(interior):
                        tt('dve', PT[qb], expS[qb], E[pi], ALU.mult, [sl_ex[qb], sl_E[pi]], [sl_PT[qb]])
                    else:
                        for ch in range(2):
                            if interior[ch]:
                                tt('dve', PT[qb][:, ch * 256:(ch + 1) * 256], expS[qb][:, ch * 256:(ch + 1) * 256], E[pi][:, ch * 256:(ch + 1) * 256], ALU.mult,
                                   [sl_ex[qb], sl_E[pi]], [sl_PT[qb]])
                            else:
                                stt('dve', PT[qb][:, ch * 256:(ch + 1) * 256], expS[qb][:, ch * 256:(ch + 1) * 256], kvalid[:, cols[ch]:cols[ch] + 1],
                                    E[pi][:, ch * 256:(ch + 1) * 256], ALU.mult, ALU.mult, [sl_ex[qb], sl_E[pi], sl_const], [sl_PT[qb]])
                    info[i] = qb

                def emit_pv(i):
                    r, qt = qtiles[i]
                    qb = info[i]; qs = i % 4
                    for hd, (Vx, slV, bank) in enumerate(((VA, sl_VA, 5), (VB, sl_VB, 6))):
                        for ch in range(2):
                            g = 2 * ch + hd
                            mm(pbank[bank][:, qs * 128:(qs + 1) * 128], Vx[:, vidx[(r, qt + ch)], :], PT[qb][:, g * 128:(g + 1) * 128], ch == 0, ch == 1,
                               [slV, sl_PT[qb]], [pslot[bank]])
                    if qs == 3 or i == ntile - 1:
                        g0 = i - qs; grp = qtiles[g0:i + 1]; n = len(grp)
                        if d == 1:
                            qt0 = grp[0][1]
                            av = accA[:, qt0 * 128:(qt0 + n) * 128]; bv = accB[:, qt0 * 128:(qt0 + n) * 128]
                            pa = pbank[5][:, 0:n * 128]; pb_ = pbank[6][:, 0:n * 128]
                        else:
                            qt_ = grp[0][1]; r0 = grp[0][0]
                            assert all(g_[1] == qt_ for g_ in grp) and n == 4
                            av = accA[:, qt_ * 128 * d:(qt_ + 1) * 128 * d].rearrange("p (i r) -> p r i", r=d)[:, r0:r0 + 4, :]
                            bv = accB[:, qt_ * 128 * d:(qt_ + 1) * 128 * d].rearrange("p (i r) -> p r i", r=d)[:, r0:r0 + 4, :]
                            pa = pbank[5][:, :].rearrange("p (a b) -> p a b", b=128); pb_ = pbank[6][:, :].rearrange("p (a b) -> p a b", b=128)
                        if pi == 0:
                            cp('act', av, pa, [pslot[5]], [sl_aA]); cp('dve', bv, pb_, [pslot[6]], [sl_aB])
                        else:
                            tt('dve', av, av, pa, ALU.add, [pslot[5], sl_aA], [sl_aA])
                            tt('dve', bv, bv, pb_, ALU.add, [pslot[6], sl_aB], [sl_aB])

                LA = 2
                for i in range(ntile + LA):
                    if i < ntile:
                        emit_qk(i)
                    if i >= LA and OPT.get('att_stop', 99) > 4:
                        emit_pv(i - LA)
            if OPT.get('att_stop', 99) <= 5: continue
            P.dma('sp', Dn[0:64, :], accA[64:128, :], [sl_aA], [sl_VB])
            P.dma('sp', Dn[64:128, :], accB[0:64, :], [sl_aB], [sl_VB])
            P.op('dve', lambda e: e.reciprocal(out=Dn, in_=Dn), [sl_VB], [sl_VB])
            tt('dve', Dn[0:64, :], accA[0:64, :], Dn[0:64, :], ALU.mult, [sl_aA, sl_VB], [sl_VB])
            tt('dve', Dn[64:128, :], accB[64:128, :], Dn[64:128, :], ALU.mult, [sl_aB, sl_VB], [sl_VB])
            cp('act', yb, Dn, [sl_VB], [sl_VA])
            P.dma('sp', yT[512 + pair * 128:512 + (pair + 1) * 128, half * QH:(half + 1) * QH], yb, [sl_VA], ())
            for j in range(QH // 512):
                act(sqb, Dn[:, j * 512:(j + 1) * 512], AF.Square, [sl_VB], [sl_sqb])
                mm(pbank[3][:, :], ones[:], sqb, True, True, [sl_sqb, sl_const], [pslot[3]])
                cp('act', bst, pbank[3][:, :], [pslot[3]], [sl_bst])
                P.dma('sp', ssqd[2 * pair + half:2 * pair + half + 1, j * 512:(j + 1) * 512], bst[0:1, :], [sl_bst], ())
            if debug and pair == 0 and half == 0:
                P.dma('sp', dout("d_yatt0", [128, QH]), Dn, [sl_VB], ())
        P.barrier()

    if 'att' in stages:
        for pair in range(OPT.get('att_pairs', 4)):
            att_pair(pair)

    if 'p2' in stages:
        P.barrier()
        T2 = 512; NH = 11
        hnb = hn[:].rearrange("p a b -> p (a b)")
        off = [0]

        def hget(shape, dt):
            esz = 4 if dt == F32 else 2
            n = int(np.prod(shape[1:])) * esz
            a = off[0] // 2; b = (off[0] + n) // 2; off[0] += n
            assert off[0] <= 98304, off[0]
            v = hnb[:, a:b]
            if dt == F32: v = v.bitcast(F32)
            if len(shape) == 3: v = v.rearrange("p (a b) -> p a b", b=shape[2])
            return v
        off[0] = 16 * DFF * 2
        h2 = hget([128, 8, T2], BF16)
        cv = Carver()
        Wd = cv.get([128, NFF, D], BF16)
        Wo = cv.get([128, 8, D], BF16)
        xres = cv.get([128, 8, T2], F32)
        wpflat = WP[:].rearrange("p a b -> p (a b)")
        actb = wpflat.rearrange("p (a b) -> p a b", b=T2)
        ytile = wpflat[:, 0:8 * T2].rearrange("p (a b) -> p a b", b=T2)
        rs_t = sb("rs_t", [128, T2]); rowt = sb("rowt", [4, T2]); sgl = sb("sgl", [128, 2, T2], BF16)
        onesf = sb("onesf", [4, 128])
        sl_Wd = Slot(); sl_Wo = Slot()
        sl_x = Slot(); sl_h2 = Slot(); sl_rs = Slot(); sl_row = Slot(); sl_sgl = [Slot(), Slot()]
        sl_wpb = [Slot() for _ in range(NH)]
        memset('dve', onesf[:], 1.0, [sl_const])
        if OPT.get('att_pairs', 4) == 0 or 'att' not in stages:
            prefetch_gate_weights()
        wout_v = w_out.rearrange("(kc p) n -> p kc n", p=128)
        wd_v = w_down.rearrange("(kc p) n -> p kc n", p=128)
        ones8 = v_tmp[:, 0:8]
        memset('dve', ones8, 1.0, [sl_small])
        cp('dve', v_tmp[:, 4:8], v_anw, [sl_small], [sl_small])
        for kc in range(8):
            load_cast(wout_v[:, kc, :], D, Wo[:, kc, :], ones8[:, kc:kc + 1], sl_Wo, ['act', 'dve'])
        memset('dve', v_tmp[:, 8:9], 1.0, [sl_small])
        bgu = sb("bgu", [128, 2 * NFF])
        sl_bgu = Slot()
        for wi, (Wt, slw) in enumerate(((Wg, sl_Wg), (Wu, sl_Wu))):
            for j in range(NFF):
                for kc in range(8):
                    mm(pbank[7][:, 128 + wi * NFF + j:129 + wi * NFF + j], Wt[:, kc, j * 128:(j + 1) * 128], sg2b[:, kc:kc + 1], kc == 0, kc == 7,
                       [slw, sl_small], [pslot[7]])
        cp('dve', bgu[:], pbank[7][:, 128:128 + 2 * NFF], [pslot[7]], [sl_bgu])
        xw_o = xw.rearrange("(c p) t -> p c t", p=128)
        yT_v = yT.rearrange("(c p) t -> p c t", p=128)
        outT_v = outT.rearrange("(c p) t -> p c t", p=128)
        sl_xc = [Slot() for _ in range(8)]; sl_h2a = Slot(); sl_h2b = Slot()
        sl_hh = [sl_h2a, sl_h2a, sl_h2a, sl_h2a, sl_h2b, sl_h2b, sl_h2b, sl_h2b]
        rs_a = sgl[:].rearrange("p a b -> p (a b)").bitcast(F32)
        NT = OWN // T2

        def load_rows(tile):
            o0 = tile * T2; half = o0 // QH; oh = o0 % QH
            P.dma('sp', rowt[:], ssqd[half:8:2, oh:oh + T2], (), [sl_row])

        def load_x(tile, oc):
            o0 = tile * T2
            P.dma('pool', xres[:, oc, :], xw_o[:, oc, HALO + o0:HALO + o0 + T2], (), [sl_xc[oc]])

        def prologue(tile):
            o0 = tile * T2
            P.dma('sp', ytile, yT_v[:, :, o0:o0 + T2], (), sl_wpb[0:8] + (sl_stg[0:2] if tile == 0 else []))
            mm(pbank[7][:, :], onesf[:], rowt[:], True, True, [sl_row, sl_const], [pslot[7]])
            act(rs_a, pbank[7][:, :], AF.Ln, [pslot[7], sl_small], sl_sgl, scale=1.0 / 512, bias=v_eps)
            act(rs_a, rs_a, AF.Exp, sl_sgl, sl_sgl, scale=-0.5)
            tt('dve', ytile[:, 4:8, :], ytile[:, 4:8, :], rs_a.unsqueeze(1).to_broadcast([128, 4, T2]), ALU.mult, sl_wpb[0:8] + sl_sgl, sl_wpb[0:8])

        load_rows(0)
        for oc in range(8):
            load_x(0, oc)
        prologue(0)
        h2f = h2[:].rearrange("p a b -> p (a b)").bitcast(F32) if hasattr(h2, 'rearrange') else None
        wstf = wst[:].rearrange("p a b c -> p (a b c)")
        ring_d = [(h2f[:, 0:1024], Slot(), [sl_h2a]), (h2f[:, 1024:2048], Slot(), [sl_h2b]),
                  (wstf[:, 0:1024], Slot(), [sl_wst[0], sl_wst[1], sl_stg[2]]), (wstf[:, 1024:2048], Slot(), [sl_wst[0], sl_wst[1], sl_stg[2]])]
        for j in range(NFF):
            load_cast(wd_v[:, j, :], D, Wd[:, j, :], v_tmp[:, 8:9], sl_Wd, ['act', 'dve'], ring=ring_d)
        for tile in range(NT):
            o0 = tile * T2
            for oc in range(8):
                pb_ = oc % 6
                for kc in range(8):
                    mm(pbank[pb_][:, :], Wo[:, kc, oc * 128:(oc + 1) * 128], ytile[:, kc, :], kc == 0, kc == 7, [sl_Wo] + sl_wpb[0:8], [pslot[pb_]])
                stt('dve', xres[:, oc, :], pbank[pb_][:, :], m_g1[:, oc:oc + 1], xres[:, oc, :], ALU.mult, ALU.add, [pslot[pb_], sl_small, sl_xc[oc]], [sl_xc[oc]])
            act(h2[:, 0:4, :], xres[:, 0:4, :], AF.Square, sl_xc[0:4], [sl_h2a])
            tt('dve', h2[:, 4:8, :], xres[:, 4:8, :], xres[:, 4:8, :], ALU.mult, sl_xc[4:8], [sl_h2b])
            for c in range(8):
                mm(pbank[6][:, :], ones[:], h2[:, c, :], c == 0, c == 7, [sl_hh[c], sl_const], [pslot[6]])
            act(rs_t[:], pbank[6][:, :], AF.Ln, [pslot[6], sl_small], [sl_rs], scale=1.0 / D, bias=v_eps)
            act(rs_t[:], rs_t[:], AF.Exp, [sl_rs], [sl_rs], scale=-0.5)
            tt('dve', h2[:, 0:4, :], xres[:, 0:4, :], rs_t[:].unsqueeze(1).to_broadcast([128, 4, T2]), ALU.mult, sl_xc + [sl_rs], [sl_h2a])
            tt('dve', h2[:, 4:8, :], xres[:, 4:8, :], rs_t[:].unsqueeze(1).to_broadcast([128, 4, T2]), ALU.mult, sl_xc + [sl_rs], [sl_h2b])
            if tile + 1 < NT:
                load_rows(tile + 1)
            for hh_ in range(2):
                for jj in range(NH):
                    j = hh_ * NH + jj
                    pg = jj % 2; pu = 2 + jj % 2
                    for kc in range(8):
                        mm(pbank[pg][:, :], Wg[:, kc, j * 128:(j + 1) * 128], h2[:, kc, :], kc == 0, kc == 7, [sl_Wg, sl_h2a if kc < 4 else sl_h2b], [pslot[pg]])
                    for kc in range(8):
                        mm(pbank[pu][:, :], Wu[:, kc, j * 128:(j + 1) * 128], h2[:, kc, :], kc == 0, kc == 7, [sl_Wu, sl_h2a if kc < 4 else sl_h2b], [pslot[pu]])
                    act(sgl[:, jj % 2, :], pbank[pg][:, :], AF.Silu, [pslot[pg], sl_bgu], [sl_sgl[jj % 2]], bias=bgu[:, j:j + 1])
                    stt('dve', actb[:, jj, :], pbank[pu][:, :], bgu[:, NFF + j:NFF + j + 1], sgl[:, jj % 2, :], ALU.add, ALU.mult,
                        [pslot[pu], sl_bgu, sl_sgl[jj % 2]], [sl_wpb[jj]])
                for oc in range(8):
                    pd_ = 4 + oc % 2
                    for jj in range(NH):
                        j = hh_ * NH + jj
                        mm(pbank[pd_][:, :], Wd[:, j, oc * 128:(oc + 1) * 128], actb[:, jj, :], jj == 0, jj == NH - 1, [sl_Wd, sl_wpb[jj]], [pslot[pd_]])
                    stt('dve', xres[:, oc, :], pbank[pd_][:, :], m_g2[:, oc:oc + 1], xres[:, oc, :], ALU.mult, ALU.add, [pslot[pd_], sl_small, sl_xc[oc]], [sl_xc[oc]])
            if tile + 1 < NT:
                prologue(tile + 1)
            act(h2[:, 0:4, :], xres[:, 0:4, :], AF.Square, sl_xc[0:4], [sl_h2a])
            tt('dve', h2[:, 4:8, :], xres[:, 4:8, :], xres[:, 4:8, :], ALU.mult, sl_xc[4:8], [sl_h2b])
            for c in range(8):
                mm(pbank[6][:, :], ones[:], h2[:, c, :], c == 0, c == 7, [sl_hh[c], sl_const], [pslot[6]])
            act(rs_t[:], pbank[6][:, :], AF.Ln, [pslot[6], sl_small], [sl_rs], scale=1.0 / D, bias=v_eps)
            act(rs_t[:], rs_t[:], AF.Exp, [sl_rs], [sl_rs], scale=-0.5)
            for oc in range(8):
                stt('dve', xres[:, oc, :], xres[:, oc, :], v_fnw[:, oc:oc + 1], rs_t[:], ALU.mult, ALU.mult, [sl_xc[oc], sl_small, sl_rs], [sl_xc[oc]])
                P.dma('sp', outT_v[:, oc, o0:o0 + T2], xres[:, oc, :], [sl_xc[oc]], ())
                if tile + 1 < NT:
                    load_x(tile + 1, oc)
    if debug:
        P.barrier()
        P.dma('sp', dout("d_yT", [D, OWN], BF16), yT, (), ())
    P.barrier()
    P.finish()
    for cm in reversed(ctxs):
        cm.__exit__(None, None, None)
    return nc, list(dbg.keys()), P


def _core_inputs(inputs, consts, core):
    b = core // 4; tq = core % 4; s0 = tq * OWN
    x = inputs['x'][b]
    lo = s0 - HALO
    idx = np.arange(lo, lo + WL)
    valid = (idx >= 0) & (idx < S)
    xwin = np.zeros((WL, D), np.float32)
    xwin[valid] = x[idx[valid]]
    m = {}
    m['xw'] = np.ascontiguousarray(xwin.T)
    m['cT'] = np.ascontiguousarray(inputs['c'][b].reshape(8, 128).T)
    vf = valid.astype(np.float32)
    m['vvalid'] = np.ascontiguousarray(vf.reshape(48, 128).T)
    kv = np.zeros((128, NKV), np.float32)
    for (half, d, r, kc), (w0, ci) in ATT_TILES.items():
        kv[:, ci] = vf[w0 + np.arange(128) * d]
    m['kvalid'] = kv
    m.update(consts)
    return m


def _shared_inputs(inputs):
    f = lambda a: np.ascontiguousarray(np.asarray(a, np.float32))
    m = {}
    m['w_ada'] = f(inputs['w_ada'][0]); m['b_adaT'] = f(inputs['b_ada'][0].reshape(48, 128).T)
    m['n1w'] = f(inputs['norm1_w'][0].reshape(8, 128).T); m['w_in'] = f(inputs['w_in'][0])
    lbr = np.asarray(inputs['hg_lower_bound'], np.float32)
    m['lbA'] = f(lbr.reshape(2, 2, 4, 128).transpose(3, 2, 1, 0).reshape(128, 16))
    m['gnw'] = f(inputs['hg_norm_w'][0].reshape(4, 128).T); m['anw'] = f(inputs['attn_norm_w'][0].reshape(4, 128).T)
    m['w_out'] = f(inputs['w_out'][0]); m['n2w'] = f(inputs['norm2_w'][0].reshape(8, 128).T)
    m['w_gate'] = f(inputs['w_gate'][0]); m['w_up'] = f(inputs['w_up'][0]); m['w_down'] = f(inputs['w_down'][0])
    m['fnw'] = f(inputs['final_norm_w'].reshape(8, 128).T)
    m['biasT'] = _bias_tables(np.asarray(inputs['rel_bias'], np.float32))
    return m


_CACHE = {}


def kernel(**inputs):
    inputs = {k: np.asarray(v) for k, v in inputs.items()}
    if 'nc' not in _CACHE:
        _CACHE['nc'] = build_program()[0]
    nc = _CACHE['nc']
    consts = _host_consts()
    shared = _shared_inputs(inputs)
    in_maps = []
    for core in range(NCORE):
        m = _core_inputs(inputs, consts, core)
        m.update(shared)
        in_maps.append(m)
    res = run_bass_kernel_spmd(nc, in_maps, core_ids=list(range(NCORE)))
    out = np.zeros((2, S, D), np.float32)
    for core in range(NCORE):
        b = core // 4; s0 = (core % 4) * OWN
        out[b, s0:s0 + OWN] = np.asarray(res.results[core]['outT']).T
    return out
```

```python
import numpy as np
import ml_dtypes
import concourse.bass as bass
import concourse.mybir as mybir
from concourse.bass_utils import run_bass_kernel_spmd

F32 = mybir.dt.float32
BF16 = mybir.dt.bfloat16
ALU = mybir.AluOpType
AF = mybir.ActivationFunctionType

S = 16384; D = 1024; WL = 6144; HALO = 1024; OWN = 4096; NCORE = 8
DFF = 2816; NFF = 22
EPS = 1e-6
PATTERNS = (1, 4, 16)
QH = 2048
TT2 = 256
OPT = {}
PV = 'dve'


class Slot:
    __slots__ = ('w', 'r', 'frozen')

    def __init__(self):
        self.w = None; self.r = {}; self.frozen = False


class Prog:
    ENG = ('sp', 'act', 'pe', 'dve', 'pool')
    NDMASEM = 8

    def __init__(self, nc):
        self.nc = nc
        self.streams = {e: [] for e in self.ENG}
        self.cnt = {e: 0 for e in self.ENG}
        self.seen = {e: {} for e in self.ENG}
        self.last = {e: None for e in self.ENG}
        self.sem = {}; self.dma_sems = {}
        self.dma_n = {e: 0 for e in self.ENG}
        self.dma_tok = {e: [] for e in self.ENG}
        self.open_dma = []
        self._ctx = []
        self.nins = 0
        for e in ('act', 'pe', 'dve', 'pool'):
            cm = nc.semaphore("s_" + e); self.sem[e] = cm.__enter__(); self._ctx.append(cm)
        for e in ('sp', 'act', 'pool'):
            l = []
            for i in range(self.NDMASEM):
                cm = nc.semaphore(f"d_{e}{i}"); l.append(cm.__enter__()); self._ctx.append(cm)
            self.dma_sems[e] = l

    def _waits(self, eng, deps):
        best = {}
        for d in deps:
            if d is None:
                continue
            sem, val, src = d
            if src == eng and eng == 'pe':
                continue
            k = id(sem)
            if self.seen[eng].get(k, 0) < val and best.get(k, (None, 0))[1] < val:
                best[k] = (sem, val)
        w = []
        for k, (sem, val) in best.items():
            self.seen[eng][k] = val
            w.append((sem, val))
        return w

    def _deps(self, reads, writes, extra):
        deps = list(extra)
        for s in reads:
            if s.w is not None:
                deps.append(s.w)
        for s in writes:
            deps.extend(s.r.values())
            if s.w is not None:
                deps.append(s.w)
        return deps

    def _update(self, tok, reads, writes):
        for s in reads:
            if not s.frozen:
                k = id(tok[0])
                if k not in s.r or s.r[k][1] < tok[1]:
                    s.r[k] = tok
        for s in writes:
            s.w = tok; s.r = {}

    def op(self, eng, fn, reads=(), writes=(), extra=()):
        w = self._waits(eng, self._deps(reads, writes, extra))
        self.cnt[eng] += 1
        tok = (self.sem[eng], self.cnt[eng], eng)
        sem = self.sem[eng]

        def run(e, fn=fn, w=w, sem=sem):
            for (s, v) in w:
                e.wait_ge(s, v)
            fn(e).then_inc(sem, 1)
        self.streams[eng].append(run)
        self.nins += 1 + len(w)
        self.last[eng] = tok
        self._update(tok, reads, writes)
        return tok

    def dma(self, eng, out, in_, reads=(), writes=(), extra=()):
        j = self.dma_n[eng]; self.dma_n[eng] += 1
        sem = self.dma_sems[eng][j % self.NDMASEM]
        val = 16 * (j // self.NDMASEM + 1)
        deps = self._deps(reads, writes, extra)
        if j >= self.NDMASEM:
            deps.append(self.dma_tok[eng][j - self.NDMASEM])
        w = self._waits(eng, deps)

        def run(e, w=w, sem=sem, out=out, in_=in_):
            for (s, v) in w:
                e.wait_ge(s, v)
            e.dma_start(out=out, in_=in_).then_inc(sem, 16)
        self.streams[eng].append(run)
        self.nins += 1 + len(w)
        tok = (sem, val, 'dma_' + eng)
        self.dma_tok[eng].append(tok)
        self.open_dma.append(tok)
        self._update(tok, reads, writes)
        return tok

    def wait(self, eng, deps):
        w = self._waits(eng, deps)

        def run(e, w=w):
            for (s, v) in w:
                e.wait_ge(s, v)
        self.streams[eng].append(run)

    def barrier(self):
        toks = [t for t in self.last.values() if t is not None] + self.open_dma
        self.open_dma = []
        for e in self.ENG:
            self.wait(e, toks)

    def finish(self):
        nc = self.nc
        with nc.Block() as block:
            @block.sync
            def _(e):
                for f in self.streams['sp']: f(e)

            @block.scalar
            def _(e):
                for f in self.streams['act']: f(e)

            @block.tensor
            def _(e):
                for f in self.streams['pe']: f(e)

            @block.vector
            def _(e):
                for f in self.streams['dve']: f(e)

            @block.gpsimd
            def _(e):
                for f in self.streams['pool']: f(e)
        for cm in reversed(self._ctx):
            cm.__exit__(None, None, None)


def _t5_bucket(rel):
    half = 16; max_exact = 8
    ret = np.where(rel > 0, half, 0)
    n = np.abs(rel)
    nf = np.maximum(n, 1).astype(np.float32)
    large = max_exact + (np.log(nf / max_exact) / np.float32(np.log(1024 / max_exact)) * (half - max_exact)).astype(np.int32)
    large = np.minimum(large, half - 1)
    return ret + np.where(n < max_exact, n, large)


def _delta_tile():
    kk = np.arange(128)[:, None]; qq = np.arange(128)[None, :]
    dA = kk - 64 - qq; dB = kk + 64 - qq
    return np.concatenate([dA, dA, dB, dB], axis=1)


def _att_geometry():
    tiles = {}
    for half in range(2):
        for d in PATTERNS:
            nq = QH // (128 * d)
            for r in range(d):
                for kc in range(nq + 1):
                    mk0 = (HALO + QH * half) // d - 64 + 128 * kc
                    tiles[(half, d, r, kc)] = (mk0 * d + r, len(tiles))
    return tiles


ATT_TILES = _att_geometry()
NKV = len(ATT_TILES)


def _host_consts():
    c = {}
    c['ident'] = np.eye(128, dtype=np.float32).astype(ml_dtypes.bfloat16)
    p = np.arange(128)[:, None] % 64; t = np.arange(256)[None, :] % 64
    hm = np.stack([(p <= t), (p >= t)], axis=1).astype(np.float32)
    c['hmask'] = hm.astype(ml_dtypes.bfloat16)
    pp = np.arange(128)[:, None]; cc = np.arange(512)[None, :]
    rowok = (pp // 64) == ((cc // 64) % 2)
    sidx = pp % 64; tidx = cc % 64
    hm2 = np.stack([rowok & (sidx <= tidx), rowok & (sidx >= tidx)], axis=1).astype(np.float32)
    c['hmask2'] = hm2.astype(ml_dtypes.bfloat16)
    c['pmask'] = np.stack([(np.arange(128) < 64), (np.arange(128) >= 64)], axis=1).astype(np.float32)
    rm = np.ones((128, 512), np.float32); rm[:, 0::64] = 0.0
    c['resetm'] = rm
    dl = _delta_tile()
    c['band'] = (np.abs(dl) <= 64).astype(np.float32).astype(ml_dtypes.bfloat16)
    return c


def _bias_tables(rel_bias):
    dl = _delta_tile()
    out = np.zeros((3, 4, 128, 512), np.float32)
    hd = np.repeat(np.array([0, 1, 0, 1]), 128)[None, :]
    for pi, d in enumerate(PATTERNS):
        bk = _t5_bucket(dl * d)
        for pair in range(4):
            out[pi, pair] = rel_bias[bk, 2 * pair + hd]
    return out


def build_program(debug=False, stages=('x', 'hg', 'att', 'p2')):
    nc = bass.Bass("TRN2", target_bir_lowering=False)

    def din(name, shape, dt=F32):
        return nc.dram_tensor(name, list(shape), dt, kind="ExternalInput").ap()

    xw = din("xw", [D, WL]); cT = din("cT", [128, 8])
    w_ada = din("w_ada", [D, 6 * D]); b_ada = din("b_adaT", [128, 48])
    n1w = din("n1w", [128, 8]); w_in = din("w_in", [D, 4096])
    lbA = din("lbA", [128, 16])
    gnw = din("gnw", [128, 4]); anw = din("anw", [128, 4])
    w_out = din("w_out", [D, D]); n2w = din("n2w", [128, 8])
    w_gate = din("w_gate", [D, DFF]); w_up = din("w_up", [D, DFF]); w_down = din("w_down", [DFF, D])
    fnw = din("fnw", [128, 8])
    ident_d = din("ident", [128, 128], BF16); hmask_d = din("hmask", [128, 2, 256], BF16)
    hmask2_d = din("hmask2", [128, 2, 512], BF16); pmask_d = din("pmask", [128, 2])
    resetm_d = din("resetm", [128, 512]); band_d = din("band", [128, 512], BF16)
    biasT = din("biasT", [3, 4, 128, 512]); kvalid_d = din("kvalid", [128, NKV]); vvalid_d = din("vvalid", [128, 48])
    outT = nc.dram_tensor("outT", [D, OWN], F32, kind="ExternalOutput").ap()
    yT = nc.dram_tensor("yT", [D, OWN], BF16).ap()
    ssqd = nc.dram_tensor("ssqd", [8, QH], F32).ap()
    dbg = {}

    def dout(name, shape, dt=F32):
        dbg[name] = nc.dram_tensor(name, list(shape), dt, kind="ExternalOutput").ap()
        return dbg[name]

    ctxs = []

    def sb(name, shape, dt=F32):
        cm = nc.sbuf_tensor(name, list(shape), dt); t = cm.__enter__(); ctxs.append(cm); return t

    def psb(name, shape, dt=F32):
        cm = nc.psum_tensor(name, list(shape), dt); t = cm.__enter__(); ctxs.append(cm); return t

    P = Prog(nc)
    hn = sb("hn", [128, 8, WL], BF16)
    ident = sb("identb", [128, 128], BF16); ones = sb("onesb", [128, 128], BF16)
    hmask2 = sb("hmask2b", [128, 2, 512], BF16)
    resetm = sb("resetmb", [128, 512]); band = sb("bandb", [128, 512], BF16)
    vvalid = sb("vvalidb", [128, 48]); kvalid = sb("kvalidb", [128, NKV])
    small = sb("small", [128, 288])
    WP = sb("WP", [128, 8, 704], BF16)
    wst = sb("wst", [128, 2, 8, 128])
    ARENA_BYTES = 79872
    arena = sb("arena", [128, ARENA_BYTES // 2], BF16)
    pbank = [psb(f"pb{i}", [128, 512]) for i in range(8)]
    pslot = [Slot() for _ in range(8)]

    class Carver:
        def __init__(self):
            self.off = 0

        def get(self, shape, dt):
            esz = 4 if dt == F32 else 2
            n = int(np.prod(shape[1:])) * esz
            a = self.off // 2; b = (self.off + n) // 2
            self.off += n
            assert self.off <= ARENA_BYTES, f"arena overflow {self.off}"
            v = arena[:, a:b]
            if dt == F32:
                v = v.bitcast(F32)
            if len(shape) == 3:
                v = v.rearrange("p (a b) -> p a b", b=shape[2])
            return v

    col = {}
    _c = [0]

    def scol(name, n=1):
        col[name] = (_c[0], n); _c[0] += n; assert _c[0] <= 288
        return small[:, col[name][0]:col[name][0] + n]

    v_c = scol('c', 8); v_e = scol('ce', 8); v_sc = scol('sc', 8)
    v_mod = scol('mod', 48); v_bada = scol('bada', 48)
    v_n1w = scol('n1w', 8); v_g1 = scol('g1', 8); v_sg1 = scol('sg1', 8); v_rg1 = scol('rg1', 8)
    v_lbA = scol('lbA', 16); v_lb = scol('lb', 8); v_oml = scol('oml', 8); v_noml = scol('noml', 8)
    v_gnw = scol('gnw', 4); v_anw = scol('anw', 4)
    v_bias = scol('bias', 5); v_nbias = scol('nbias', 5)
    v_n2w = scol('n2w', 8); v_g2 = scol('g2', 8); v_sg2 = scol('sg2', 8); v_fnw = scol('fnw', 8)
    v_eps = scol('eps', 1); v_tmp = scol('tmp', 16)
    sl_small = Slot(); sl_const = Slot(); sl_hn = [Slot() for _ in range(12)]
    sl_WP = Slot(); sl_wst = [Slot(), Slot()]; sl_bias = Slot()

    def mm(out, lhsT, rhs, start, stop, reads, writes):
        return P.op('pe', lambda e: e.matmul(out, lhsT=lhsT, rhs=rhs, start=start, stop=stop, skip_group_check=True), reads, writes)

    def act(out, in_, func, reads, writes, scale=None, bias=None, eng='act'):
        kw = {}
        if scale is not None: kw['scale'] = scale
        if bias is not None: kw['bias'] = bias
        return P.op('act', lambda e: e.activation(out=out, in_=in_, func=func, **kw), reads, writes)

    def tt(eng, out, in0, in1, op, reads, writes):
        return P.op(eng, lambda e: e.tensor_tensor(out=out, in0=in0, in1=in1, op=op), reads, writes)

    def stt(eng, out, in0, scalar, in1, op0, op1, reads, writes):
        return P.op(eng, lambda e: e.scalar_tensor_tensor(out=out, in0=in0, scalar=scalar, in1=in1, op0=op0, op1=op1), reads, writes)

    def ts(eng, out, in0, s1, s2, op0, op1, reads, writes):
        if s2 is None:
            return P.op(eng, lambda e: e.tensor_scalar(out=out, in0=in0, scalar1=s1, scalar2=None, op0=op0), reads, writes)
        return P.op(eng, lambda e: e.tensor_scalar(out=out, in0=in0, scalar1=s1, scalar2=s2, op0=op0, op1=op1), reads, writes)

    def cp(eng, out, in_, reads, writes):
        if eng == 'act':
            return P.op('act', lambda e: e.activation(out=out, in_=in_, func=AF.Copy), reads, writes)
        return P.op(eng, lambda e: e.tensor_copy(out=out, in_=in_), reads, writes)

    def memset(eng, ap, val, writes):
        return P.op(eng, lambda e: e.memset(ap, val), (), writes)

    def sigmoid_parts(eng_v, out_r, in_ap, tmp_e, reads, writes_e, writes_r):
        act(tmp_e, in_ap, AF.Exp, reads, writes_e, scale=-1.0)
        ts(eng_v, tmp_e, tmp_e, 1.0, None, ALU.add, None, writes_e, writes_e)
        P.op(eng_v, lambda e: e.reciprocal(out=out_r, in_=tmp_e), writes_e, writes_r)

    P.dma('sp', ident[:], ident_d, (), [sl_const])
    P.dma('sp', hmask2[:], hmask2_d, (), [sl_const])
    P.dma('sp', resetm[:], resetm_d, (), [sl_const]); P.dma('sp', band[:], band_d, (), [sl_const])
    P.dma('sp', vvalid[:], vvalid_d, (), [sl_const]); P.dma('sp', kvalid[:], kvalid_d, (), [sl_const])
    P.dma('sp', v_c, cT, (), [sl_small]); P.dma('sp', v_bada, b_ada, (), [sl_small])
    P.dma('sp', v_n1w, n1w, (), [sl_small]); P.dma('sp', v_lbA, lbA, (), [sl_small])
    P.dma('sp', v_gnw, gnw, (), [sl_small]); P.dma('sp', v_anw, anw, (), [sl_small])
    P.dma('sp', v_n2w, n2w, (), [sl_small]); P.dma('sp', v_fnw, fnw, (), [sl_small])
    memset('pool', ones[:], 1.0, [sl_const]); memset('pool', v_eps, EPS, [sl_small])
    sigmoid_parts('dve', v_sc, v_c, v_e, [sl_small], [sl_small], [sl_small])
    tt('dve', v_sc, v_sc, v_c, ALU.mult, [sl_small], [sl_small])
    cv0 = Carver()
    wada_st = [cv0.get([128, 8, 512], F32), cv0.get([128, 8, 512], F32)]
    sl_wada = [Slot(), Slot()]
    xt = [cv0.get([128, 8, 512], F32), cv0.get([128, 8, 512], F32)]
    sq = cv0.get([128, 8, 512], BF16)
    rstd = cv0.get([128, 512], F32); lnv = cv0.get([128, 512], F32)
    sl_xt = [Slot(), Slot()]; sl_sq = Slot(); sl_sq2 = Slot(); sl_rstd = Slot(); sl_ln = Slot()
    xw_v = xw.rearrange("(c p) t -> p c t", p=128)
    w_ada_v = w_ada.rearrange("(kc p) n -> p kc n", p=128)

    def rstd_from_psum(ps_ap, ps_slot, nfeat, out_ap, out_slot, ln_ap, ln_slot):
        act(ln_ap, ps_ap, AF.Ln, [ps_slot, sl_small], [ln_slot], scale=1.0 / nfeat, bias=v_eps)
        act(out_ap, ln_ap, AF.Exp, [ln_slot], [out_slot], scale=-0.5)

    order = [2, 3, 0, 1] + list(range(4, 12))
    for ii, jt in enumerate(order):
        b = ii % 2
        P.dma('sp', wada_st[b], w_ada_v[:, :, jt * 512:(jt + 1) * 512], (), [sl_wada[b]])
        t = ii
        P.dma('sp', xt[b], xw_v[:, :, t * 512:(t + 1) * 512], (), [sl_xt[b]])
        for jj in range(4):
            j = jt * 4 + jj
            for kc in range(8):
                mm(pbank[7][:, j:j + 1], wada_st[b][:, kc, jj * 128:(jj + 1) * 128], v_sc[:, kc:kc + 1], kc == 0, kc == 7,
                   [sl_wada[b], sl_small], [pslot[7]])
        act(sq[:, 0:4, :], xt[b][:, 0:4, :], AF.Square, [sl_xt[b]], [sl_sq])
        tt('dve', sq[:, 4:8, :], xt[b][:, 4:8, :], xt[b][:, 4:8, :], ALU.mult, [sl_xt[b]], [sl_sq2])
        pb = t % 2
        for c in range(8):
            mm(pbank[pb][:, :], ones[:], sq[:, c, :], c == 0, c == 7, [sl_sq if c < 4 else sl_sq2, sl_const], [pslot[pb]])
        rstd_from_psum(pbank[pb][:, :], pslot[pb], D, rstd, sl_rstd, lnv, sl_ln)
        tt('dve', hn[:, :, t * 512:(t + 1) * 512], xt[b], rstd.unsqueeze(1).to_broadcast([128, 8, 512]), ALU.mult,
           [sl_xt[b], sl_rstd], [sl_hn[t]])
    for s_ in sl_hn:
        s_.frozen = True
    tt('dve', v_mod, pbank[7][:, 0:48], v_bada, ALU.add, [pslot[7], sl_small], [sl_small])
    m_sh1 = v_mod[:, 0:8]; m_sc1 = v_mod[:, 8:16]; m_g1 = v_mod[:, 16:24]
    m_sh2 = v_mod[:, 24:32]; m_sc2 = v_mod[:, 32:40]; m_g2 = v_mod[:, 40:48]
    stt('dve', v_g1, m_sc1, 1.0, v_n1w, ALU.add, ALU.mult, [sl_small], [sl_small])
    P.op('dve', lambda e: e.reciprocal(out=v_rg1, in_=v_g1), [sl_small], [sl_small])
    tt('dve', v_sg1, m_sh1, v_rg1, ALU.mult, [sl_small], [sl_small])
    stt('dve', v_g2, m_sc2, 1.0, v_n2w, ALU.add, ALU.mult, [sl_small], [sl_small])
    P.op('dve', lambda e: e.reciprocal(out=v_tmp[:, 0:8], in_=v_g2), [sl_small], [sl_small])
    tt('dve', v_sg2, m_sh2, v_tmp[:, 0:8], ALU.mult, [sl_small], [sl_small])
    lbv = v_lbA.rearrange("p (a l) -> p a l", l=2)
    tt('dve', v_tmp[:, 0:8], lbv[:, :, 1], lbv[:, :, 0], ALU.subtract, [sl_small], [sl_small])
    act(v_tmp[:, 8:16], v_tmp[:, 0:8], AF.Exp, [sl_small], [sl_small])
    ts('dve', v_tmp[:, 8:16], v_tmp[:, 8:16], 1.0, None, ALU.add, None, [sl_small], [sl_small])
    P.op('dve', lambda e: e.reciprocal(out=v_lb, in_=v_tmp[:, 8:16]), [sl_small], [sl_small])
    ts('dve', v_oml, v_lb, -1.0, 1.0, ALU.mult, ALU.add, [sl_small], [sl_small])
    ts('dve', v_noml, v_lb, 1.0, -1.0, ALU.mult, ALU.add, [sl_small], [sl_small])
    zt = Carver().get([128, 512], BF16); sl_zt = Slot()
    memset('dve', zt, 0.0, [sl_zt])
    mm(pbank[3][:, :], zt[:, 0:128], zt, True, True, [sl_zt], [pslot[3]]); mm(pbank[4][:, :], zt[:, 0:128], zt, True, True, [sl_zt], [pslot[4]])
    sg1b = sb("sg1b", [128, 8], BF16); sg2b = sb("sg2b", [128, 8], BF16)
    cp('dve', sg1b[:], v_sg1, [sl_small], [sl_small]); cp('dve', sg2b[:], v_sg2, [sl_small], [sl_small])
    if debug:
        P.dma("sp", dout("d_small", [128, 288]), small[:], [sl_small], ())
    P.barrier()

    if debug:
        P.dma('sp', dout("d_hn", [128, 8, WL], BF16), hn[:], sl_hn, ())
    P.barrier()

    w_in_v = w_in.rearrange("(kc p) n -> p kc n", p=128)
    wst_n = [0]

    def prep_group(col0s):
        for i, c0 in enumerate(col0s):
            b = wst_n[0] % 2; wst_n[0] += 1
            P.dma('sp', wst[:, b], w_in_v[:, :, c0:c0 + 128], (), [sl_wst[b]])
            tt('dve', WP[:, :, i * 128:(i + 1) * 128], wst[:, b], v_g1.unsqueeze(2).to_broadcast([128, 8, 128]), ALU.mult,
               [sl_wst[b], sl_small], [sl_WP])
        for i in range(len(col0s)):
            for kc in range(8):
                mm(pbank[7][:, 64 + i:65 + i], WP[:, kc, i * 128:(i + 1) * 128], sg1b[:, kc:kc + 1], kc == 0, kc == 7,
                   [sl_WP, sl_small], [pslot[7]])
        n = len(col0s)
        cp('dve', v_bias[:, 0:n], pbank[7][:, 64:64 + n], [pslot[7]], [sl_bias])
        ts('dve', v_nbias[:, 0:n], pbank[7][:, 64:64 + n], -1.0, None, ALU.mult, None, [pslot[7]], [sl_bias])

    ipn = [0]

    def inproj(i, t0, n, bank=None):
        if bank is None:
            pb = ipn[0] % 2; ipn[0] += 1
        else:
            pb = bank
        rs = [sl_WP] + [sl_hn[k] for k in range(t0 // 512, (t0 + n - 1) // 512 + 1)]
        for kc in range(8):
            mm(pbank[pb][:, 0:n], WP[:, kc, i * 128:(i + 1) * 128], hn[:, kc, t0:t0 + n], kc == 0, kc == 7, rs, [pslot[pb]])
        return pbank[pb][:, 0:n], pslot[pb]

    def hgrn_head(hh):
        prep_group([hh * 128, 512 + hh * 128, 1024 + hh * 128, 1536 + hh * 128, 2048 + hh * 128])
        cv = Carver()
        qT = cv.get([128, OWN], BF16); sgT = cv.get([128, OWN], BF16); Vtok = cv.get([128, 48, 128], BF16)
        oacc = cv.get([128, OWN], F32)
        sl_q = [Slot() for _ in range(8)]; sl_sg = [Slot() for _ in range(8)]; sl_V = [Slot() for _ in range(12)]
        sl_o = [Slot() for _ in range(8)]
        sets = []
        for _s in range(2):
            st = {}
            for n in ('X0', 'X1', 'X2', 'X3', 'X4'):
                st[n] = (cv.get([128, 512], F32), Slot())
            for n in ('A', 'B', 'EK', 'Qi'):
                st[n] = (cv.get([128, 512], BF16), Slot())
            st['KhT'] = (cv.get([128, 4, 128], BF16), Slot()); st['PT'] = (cv.get([128, 512], BF16), Slot())
            sets.append(st)
        b_vT, sl_vT = sets[0]['PT']
        Sst = [cv.get([128, 128], BF16) for _ in range(6)]
        psT = pbank[2].bitcast(BF16)
        for t in range(12):
            t0 = t * 512
            own = 2 <= t < 10
            st = sets[t % 2]
            if own:
                o0 = t0 - HALO; ot = t - 2
                (x0, s0), (x2, s2) = st['X0'], st['X2']
                ps, psl = inproj(0, t0, 512)
                ts('dve', qT[:, o0:o0 + 512], ps, v_bias[:, 0:1], None, ALU.add, None, [psl, sl_bias], [sl_q[ot]])
                ps, psl = inproj(4, t0, 512)
                act(x0, ps, AF.Exp, [psl, sl_bias], [s0], scale=-1.0, bias=v_nbias[:, 4:5])
                act(x2, x0, AF.Ln, [s0], [s2], bias=1.0)
                act(x2, x2, AF.Exp, [s2], [s2], scale=-1.0)
                stt('dve', sgT[:, o0:o0 + 512], ps, v_bias[:, 4:5], x2, ALU.add, ALU.mult, [psl, sl_bias, s2], [sl_sg[ot]])
            ps, psl = inproj(3, t0, 512)
            act(b_vT, ps, AF.Identity, [psl, sl_bias], [sl_vT], bias=v_bias[:, 3:4])
            for j in range(4):
                P.op('pe', lambda e, j=j: e.transpose(psT[:, j * 128:(j + 1) * 128], b_vT[:, j * 128:(j + 1) * 128], ident[:]),
                     [sl_vT, sl_const], [pslot[2]])
            tt('dve', Vtok[:, 4 * t:4 * t + 4, :], psT[:, 0:512].rearrange("p (a b) -> p a b", b=128),
               vvalid[:, 4 * t:4 * t + 4].unsqueeze(2).to_broadcast([128, 4, 128]), ALU.mult, [pslot[2], sl_const], [sl_V[t]])
        memset('dve', oacc, 0.0, sl_o)
        Sdir = [[Sst[0], Sst[1], Sst[2]], [Sst[3], Sst[4], Sst[5]]]
        sl_Sd = [[Slot() for _ in range(3)] for _ in range(2)]
        sl_khp = [Slot(), Slot()]; sl_ds = [[Slot(), Slot()], [Slot(), Slot()]]
        scur_ = [0, 0]; dcnt = [0, 0]
        blocks_d = [list(range(0, 10)), list(range(11, 1, -1))]
        nsteps = OPT.get('hg_nblocks', 10)
        r3 = lambda a: a.rearrange("p (c j) -> p c j", j=64)
        psT1 = pbank[1].bitcast(BF16)
        for dr in range(2):
            memset('pool', Sdir[dr][0], 0.0, [sl_Sd[dr][0]])

        def stage_A1a(dr, t):
            fi = 1 + dr
            lbc = v_lb[:, 2 * hh + dr:2 * hh + dr + 1]; omlc = v_oml[:, 2 * hh + dr:2 * hh + dr + 1]; nomlc = v_noml[:, 2 * hh + dr:2 * hh + dr + 1]
            t0 = t * 512
            st = sets[dr]
            (X0, s0), (X1, s1), (X2, s2), (X3, s3), (X4, s4) = st['X0'], st['X1'], st['X2'], st['X3'], st['X4']
            ps, psl = inproj(fi, t0, 512, bank=0)
            act(X0, ps, AF.Exp, [psl, sl_bias], [s0], scale=-1.0, bias=v_nbias[:, fi:fi + 1])
            act(X2, X0, AF.Ln, [s0], [s2], bias=1.0)
            act(X2, X2, AF.Exp, [s2], [s2], scale=-1.0)
            act(X1, X2, AF.Ln, [s2, sl_small], [s1], scale=omlc, bias=lbc)
            act(X0, X2, AF.Identity, [s2, sl_small, s0], [s0], scale=nomlc, bias=omlc)
            P.op('dve', lambda e, X3=X3, X1=X1: e.tensor_tensor_scan(out=X3, data0=resetm[:], data1=X1, initial=0.0, op0=ALU.mult, op1=ALU.add),
                 [s1, sl_const], [s3])
            if dr == 0:
                tt('dve', r3(X2), r3(X3)[:, :, 63:64].to_broadcast([128, 8, 64]), r3(X3), ALU.subtract, [s3, s2], [s2])
                Gx, sG = X3, s3
                Dx, sD = X1, s1
            else:
                tt('dve', X2, X3, X1, ALU.subtract, [s3, s1, s2], [s2])
                tt('dve', r3(X1), r3(X3)[:, :, 63:64].to_broadcast([128, 8, 64]), r3(X2), ALU.subtract, [s3, s2, s1], [s1])
                Gx, sG = X1, s1
                Dx, sD = X3, s3
            tt('dve', r3(Dx), r3(Gx), r3(Gx)[:, :, 32:33].to_broadcast([128, 8, 64]), ALU.subtract, [sG, sD], [sD])
            ts('dve', Dx, Dx, 60.0, -60.0, ALU.min, ALU.max, [sD], [sD])
            return dict(dr=dr, t=t, t0=t0, own=(2 <= t < 10), Gx=Gx, sG=sG, Dx=Dx, sD=sD)

        def stage_A1b(cx):
            dr = cx['dr']; t = cx['t']; t0 = cx['t0']; own = cx['own']; Gx = cx['Gx']; sG = cx['sG']; Dx = cx['Dx']; sD = cx['sD']
            st = sets[dr]
            (X0, s0), (X1, s1), (X2, s2), (X4, s4) = st['X0'], st['X1'], st['X2'], st['X4']
            (bA, sA), (bB, sB), (bEK, sEK), (bQi, sQi) = st['A'], st['B'], st['EK'], st['Qi']
            act(bEK, X2, AF.Exp, [s2], [sEK])
            act(X4, Gx, AF.Exp, [sG, s4], [s4])
            tt('dve', bEK, X0, bEK, ALU.mult, [s0, sEK], [sEK])
            if own:
                o0 = t0 - HALO; ot = t - 2
                act(bA, Dx, AF.Exp, [sD], [sA])
                act(bB, Dx, AF.Exp, [sD], [sB], scale=-1.0)
                tt('dve', bA, qT[:, o0:o0 + 512], bA, ALU.mult, [sl_q[ot], sA], [sA])
                tt('dve', bQi, qT[:, o0:o0 + 512], X4, ALU.mult, [sl_q[ot], s4], [sQi])
                tt('dve', bB, X0, bB, ALU.mult, [s0, sB], [sB])

        def stage_A2(cx):
            dr = cx['dr']
            st = sets[dr]
            (bEK, sEK), (bKhT, sKhT) = st['EK'], st['KhT']
            for j in range(4):
                P.op('pe', lambda e, j=j, bEK=bEK: e.transpose(psT1[:, j * 128:(j + 1) * 128], bEK[:, j * 128:(j + 1) * 128], ident[:]),
                     [sEK, sl_const], [pslot[1]])
            cp('act', bKhT, psT1[:, 0:512].rearrange("p (a b) -> p a b", b=128), [pslot[1]], [sKhT])

        def stage_B2(cxl):
            info = []
            for cx in cxl:
                dr = cx['dr']; t = cx['t']; t0 = cx['t0']; own = cx['own']
                st = sets[dr]
                (bA, sA), (bB, sB), (bPT, sPT) = st['A'], st['B'], st['PT']
                corder = list(range(8)) if dr == 0 else list(range(7, -1, -1))
                sbank = 3 + dr
                if own:
                    for c in corder:
                        rows = slice((c % 2) * 64, (c % 2) * 64 + 64)
                        mm(pbank[sbank][rows, c * 64:(c + 1) * 64], bB[:, c * 64:(c + 1) * 64], bA[:, c * 64:(c + 1) * 64], True, True,
                           [sB, sA], [pslot[sbank]])
                    tt('dve', bPT, pbank[sbank][:, :], hmask2[:, dr, :], ALU.mult, [pslot[sbank], sl_const], [sPT])
                info.append(corder)
            for hf4 in range(2):
                for cx, corder in zip(cxl, info):
                    dr = cx['dr']; t = cx['t']; t0 = cx['t0']
                    st = sets[dr]; (bKhT, sKhT) = st['KhT']
                    dbank = 2 if dr == 0 else 7
                    for ci in range(4):
                        c = corder[hf4 * 4 + ci]
                        rows = slice((c % 2) * 64, (c % 2) * 64 + 64)
                        jt = (t0 + 64 * c) // 128
                        cc = ci * 128
                        if c % 2 == 0:
                            mm(pbank[dbank][:, cc:cc + 128], bKhT[rows, c // 2, :], Vtok[rows, jt, :], True, True, [sKhT, sl_V[t]], [pslot[dbank]])
                        else:
                            for hf in range(2):
                                mm(pbank[dbank][hf * 64:(hf + 1) * 64, cc:cc + 128], bKhT[rows, c // 2, hf * 64:(hf + 1) * 64], Vtok[rows, jt, :], True, True,
                                   [sKhT, sl_V[t]], [pslot[dbank]])
                for ci in range(4):
                    for cx, corder in zip(cxl, info):
                        dr = cx['dr']; t = cx['t']; t0 = cx['t0']; own = cx['own']
                        st = sets[dr]
                        tot_col = 63 if dr == 0 else 0
                        (X4, s4) = st['X4']; (bQi, sQi) = st['Qi']; (bPT, sPT) = st['PT']
                        dbank = 2 if dr == 0 else 7; obank = 5 + dr
                        c = corder[hf4 * 4 + ci]
                        jt = (t0 + 64 * c) // 128
                        cc = ci * 128
                        scur = scur_[dr]
                        if own:
                            mm(pbank[obank][:, c * 64:(c + 1) * 64], Vtok[:, jt, :], bPT[:, c * 64:(c + 1) * 64], True, False,
                               [sl_V[t], sPT], [pslot[obank]])
                            mm(pbank[obank][:, c * 64:(c + 1) * 64], Sdir[dr][scur], bQi[:, c * 64:(c + 1) * 64], False, True,
                               [sl_Sd[dr][scur], sQi], [pslot[obank]])
                        snx = (scur + 1) % 3
                        stt('dve', Sdir[dr][snx], Sdir[dr][scur], X4[:, c * 64 + tot_col:c * 64 + tot_col + 1], pbank[dbank][:, cc:cc + 128], ALU.mult, ALU.add,
                            [sl_Sd[dr][scur], s4, pslot[dbank]], [sl_Sd[dr][snx]])
                        scur_[dr] = snx
            for cx in cxl:
                dr = cx['dr']; t = cx['t']; t0 = cx['t0']; own = cx['own']
                if own:
                    o0 = t0 - HALO; ot = t - 2; obank = 5 + dr
                    tt('dve', oacc[:, o0:o0 + 512], oacc[:, o0:o0 + 512], pbank[obank][:, :], ALU.add, [pslot[obank], sl_o[ot]], [sl_o[ot]])

        cxs = {}
        for dr in range(2):
            cxs[(dr, 0)] = stage_A1a(dr, blocks_d[dr][0])
        for dr in range(2):
            stage_A1b(cxs[(dr, 0)]); stage_A2(cxs[(dr, 0)])
        for i in range(nsteps):
            if i + 1 < nsteps:
                for dr in range(2):
                    cxs[(dr, i + 1)] = stage_A1a(dr, blocks_d[dr][i + 1])
            stage_B2([cxs[(0, i)], cxs[(1, i)]])
            if i + 1 < nsteps:
                for dr in range(2):
                    stage_A1b(cxs[(dr, i + 1)])
                for dr in range(2):
                    stage_A2(cxs[(dr, i + 1)])
        for ot in range(OPT.get('hg_fin', 8)):
            o0 = ot * 512
            st = sets[ot % 2]
            (X0, s0), (X1, s1), (X2, s2) = st['X0'], st['X1'], st['X2']
            (bA, sA), (bQi, sQi) = st['A'], st['Qi']
            act(bA, oacc[:, o0:o0 + 512], AF.Square, [sl_o[ot]], [sA])
            mm(pbank[3][:, :], ones[:], bA, True, True, [sA, sl_const], [pslot[3]])
            rstd_from_psum(pbank[3][:, :], pslot[3], 128, X0, s0, X1, s1)
            tt('dve', X2, oacc[:, o0:o0 + 512], X0, ALU.mult, [sl_o[ot], s0], [s2])
            stt('dve', bQi, X2, v_gnw[:, hh:hh + 1], sgT[:, o0:o0 + 512], ALU.mult, ALU.mult, [s2, sl_small, sl_sg[ot]], [sQi])
            P.dma('sp', yT[hh * 128:(hh + 1) * 128, o0:o0 + 512], bQi, [sQi], ())
        if debug and hh == 0:
            P.dma('sp', dout("d_ohg0", [128, OWN]), oacc, sl_o, ())
            P.dma('sp', dout("d_qT0", [128, OWN], BF16), qT, sl_q, ())
        P.barrier()

    if 'hg' in stages:
        for hh in range(OPT.get('hg_heads', 4)):
            hgrn_head(hh)

    hnb = hn[:].rearrange("p a b -> p (a b)")
    Wg = hnb[:, 0:8 * DFF].rearrange("p (a b) -> p a b", b=DFF)
    Wu = hnb[:, 8 * DFF:16 * DFF].rearrange("p (a b) -> p a b", b=DFF)
    sl_Wg = Slot(); sl_Wu = Slot()
    HW = DFF // 2
    wpf = WP[:].rearrange("p a b -> p (a b)")
    stg = [wpf[:, 0:2 * HW].bitcast(F32), wpf[:, 2 * HW:4 * HW].bitcast(F32), wst[:].rearrange("p a b c -> p (a b c)")[:, 0:HW]]
    sl_stg = [Slot(), Slot(), Slot()]
    stg_extra = [[sl_WP], [sl_WP], [sl_wst[0], sl_wst[1]]]
    stg_n = [0]
    wg_v = w_gate.rearrange("(kc p) n -> p kc n", p=128); wu_v = w_up.rearrange("(kc p) n -> p kc n", p=128)

    def load_cast(src_ap, ncol, dst_ap, scale_col, dst_slot, engs, extra=(), ring=None):
        if ring is None:
            k = stg_n[0] % 3; stg_n[0] += 1
            sbuf_, sslot_, sextra_ = stg[k], sl_stg[k], stg_extra[k]
        else:
            k = stg_n[0] % len(ring); stg_n[0] += 1
            sbuf_, sslot_, sextra_ = ring[k]
        P.dma('sp', sbuf_[:, 0:ncol], src_ap, (), [sslot_] + sextra_)
        eng = engs[stg_n[0] % len(engs)]
        rd = [sslot_, sl_small] + list(sextra_)
        stg_k = sbuf_
        if eng == 'act':
            P.op('act', lambda e: e.activation(out=dst_ap, in_=stg_k[:, 0:ncol], func=AF.Copy, scale=scale_col), rd, [dst_slot], extra)
        elif eng == 'dve':
            P.op('dve', lambda e: e.tensor_scalar(out=dst_ap, in0=stg_k[:, 0:ncol], scalar1=scale_col, scalar2=None, op0=ALU.mult), rd, [dst_slot], extra)
        else:
            P.op('pool', lambda e: e.tensor_tensor(out=dst_ap, in0=stg_k[:, 0:ncol], in1=scale_col.to_broadcast([128, ncol]), op=ALU.mult), rd, [dst_slot], extra)

    def prefetch_gate_weights():
        tok = P.last['pe']
        for kc in range(8):
            for h in range(2):
                load_cast(wg_v[:, kc, h * HW:(h + 1) * HW], HW, Wg[:, kc, h * HW:(h + 1) * HW], v_g2[:, kc:kc + 1], sl_Wg, ['pool'], extra=[tok])
        for kc in range(8):
            for h in range(2):
                load_cast(wu_v[:, kc, h * HW:(h + 1) * HW], HW, Wu[:, kc, h * HW:(h + 1) * HW], v_g2[:, kc:kc + 1], sl_Wu, ['pool'], extra=[tok])

    def att_pair(pair):
        prep_group([2560 + pair * 128, 3072 + pair * 128, 3584 + pair * 128])
        cv = Carver()
        KT = cv.get([128, WL], BF16); vT = cv.get([128, WL], BF16)
        QX = cv.get([128, 2, QH], BF16)
        VAB = cv.get([128, 2, 32, 128], BF16) if False else None
        VA = cv.get([128, 32, 128], BF16); VB = cv.get([128, 32, 128], BF16)
        accA = cv.get([128, QH], F32); accB = cv.get([128, QH], F32)
        E = [cv.get([128, 512], BF16) for _ in range(3)]
        bst = cv.get([128, 512], F32)
        expS = [cv.get([128, 512], BF16) for _ in range(3)]; PT = [cv.get([128, 512], BF16) for _ in range(3)]
        sqb = expS[0]
        Dn = VB[:].rearrange("p a b -> p (a b)").bitcast(F32)
        yb = VA[:].rearrange("p a b -> p (a b)")[:, 0:QH]
        sl_Q = Slot(); sl_K = [Slot() for _ in range(12)]; sl_v = [Slot() for _ in range(12)]
        sl_VA = Slot(); sl_VB = Slot(); sl_aA = Slot(); sl_aB = Slot(); sl_E = [Slot() for _ in range(3)]; sl_bst = Slot()
        sl_ex = [Slot(), Slot(), Slot()]; sl_PT = [Slot(), Slot(), Slot()]; sl_sqb = sl_ex[0]
        SB = [3, 4, 0, 1]
        psT = pbank[2].bitcast(BF16)
        for t in range(12):
            t0 = t * 512
            ps, psl = inproj(1, t0, 512)
            act(KT[:, t0:t0 + 512], ps, AF.Identity, [psl, sl_bias], [sl_K[t]], bias=v_bias[:, 1:2])
            ps, psl = inproj(2, t0, 512)
            ts('dve', vT[:, t0:t0 + 512], ps, v_bias[:, 2:3], None, ALU.add, None, [psl, sl_bias], [sl_v[t]])
        for pi in range(3):
            P.dma('sp', bst, biasT[pi, pair], (), [sl_bst])
            act(bst, bst, AF.Exp, [sl_bst], [sl_bst])
            tt('dve', E[pi], bst, band[:], ALU.mult, [sl_bst, sl_const], [sl_E[pi]])
        qn = [0]
        for half in range(OPT.get('att_halves', 2)):
            if OPT.get('att_stop', 99) <= 1: break
            memset('dve', QX[64:128, 0, :], 0.0, [sl_Q]); memset('dve', QX[0:64, 1, :], 0.0, [sl_Q])
            memset('dve', VA[:, :, 64:128], 1.0, [sl_VA]); memset('dve', VB[:, :, 0:64], 1.0, [sl_VB])
            for tq in range(QH // 512):
                t0 = HALO + half * QH + tq * 512
                ps, psl = inproj(0, t0, 512)
                ts('dve', QX[0:64, 0, tq * 512:(tq + 1) * 512], ps[0:64, :], v_bias[0:64, 0:1], 0.125, ALU.add, ALU.mult, [psl, sl_bias], [sl_Q])
                ts('dve', QX[64:128, 1, tq * 512:(tq + 1) * 512], ps[64:128, :], v_bias[64:128, 0:1], 0.125, ALU.add, ALU.mult, [psl, sl_bias], [sl_Q])
            if pair == OPT.get('att_pairs', 4) - 1 and half == 1 and 'p2' in stages:
                prefetch_gate_weights()
            for pi, d in enumerate(PATTERNS):
                if OPT.get('att_stop', 99) <= 2 or pi >= OPT.get('att_npat', 3): break
                if pi < OPT.get('att_pat0', 0): continue
                nq = QH // (128 * d)
                keys = [(r, kc) for r in range(d) for kc in range(nq + 1)]
                vidx = {}
                for g0 in range(0, len(keys), 4):
                    grp = keys[g0:g0 + 4]
                    for jj, (r, kc) in enumerate(grp):
                        w0, _ = ATT_TILES[(half, d, r, kc)]
                        src = vT[:, w0:w0 + 127 * d + 1:d] if d > 1 else vT[:, w0:w0 + 128]
                        tl = range(w0 // 512, (w0 + 127 * d) // 512 + 1)
                        P.op('pe', lambda e, jj=jj, src=src: e.transpose(psT[:, jj * 128:(jj + 1) * 128], src, ident[:]),
                             [sl_v[k] for k in tl] + [sl_const], [pslot[2]])
                        vidx[(r, kc)] = g0 + jj
                    n = len(grp)
                    pv3 = psT[:, 0:n * 128].rearrange("p (a b) -> p a b", b=128)
                    cp('act', VA[:, g0:g0 + n, 0:64], pv3[:, :, 0:64], [pslot[2]], [sl_VA])
                    cp('act', VB[:, g0:g0 + n, 64:128], pv3[:, :, 64:128], [pslot[2]], [sl_VB])
                qtiles = [(r, qt) for qt in range(nq) for r in range(d)]
                ntile = len(qtiles)
                info = {}
                if OPT.get('att_stop', 99) <= 3: continue

                def emit_qk(i):
                    r, qt = qtiles[i]
                    qb = qn[0] % 3; sbank = SB[qn[0] % 4]; qn[0] += 1
                    lq0 = 128 * d * qt + r
                    qsl = slice(lq0, lq0 + 127 * d + 1, d) if d > 1 else slice(lq0, lq0 + 128)
                    cols = []
                    for ch in range(2):
                        w0, kcol = ATT_TILES[(half, d, r, qt + ch)]
                        ksl = slice(w0, w0 + 127 * d + 1, d) if d > 1 else slice(w0, w0 + 128)
                        ktl = range(w0 // 512, (w0 + 127 * d) // 512 + 1)
                        mm(pbank[sbank][:, ch * 256:(ch + 1) * 256].rearrange("p (a b) -> p a b", b=128), KT[:, ksl], QX[:, :, qsl], True, True,
                           [sl_K[k] for k in ktl] + [sl_Q], [pslot[sbank]])
                        cols.append(kcol)
                    act(expS[qb], pbank[sbank][:, :], AF.Exp, [pslot[sbank]], [sl_ex[qb]])
                    interior = []
                    for ch in range(2):
                        w0, _ = ATT_TILES[(half, d, r, qt + ch)]
                        interior.append(w0 >= HALO and w0 + 127 * d < HALO + OWN)
                    if all(interior):
                        tt('dve', PT[qb], expS[qb], E[pi], ALU.mult, [sl_ex[qb], sl_E[pi]], [sl_PT[qb]])
                    else:
                        for ch in range(2):
                            if interior[ch]:
                                tt('dve', PT[qb][:, ch * 256:(ch + 1) * 256], expS[qb][:, ch * 256:(ch + 1) * 256], E[pi][:, ch * 256:(ch + 1) * 256], ALU.mult,
                                   [sl_ex[qb], sl_E[pi]], [sl_PT[qb]])
                            else:
                                stt('dve', PT[qb][:, ch * 256:(ch + 1) * 256], expS[qb][:, ch * 256:(ch + 1) * 256], kvalid[:, cols[ch]:cols[ch] + 1],
                                    E[pi][:, ch * 256:(ch + 1) * 256], ALU.mult, ALU.mult, [sl_ex[qb], sl_E[pi], sl_const], [sl_PT[qb]])
                    info[i] = qb

                def emit_pv(i):
                    r, qt = qtiles[i]
                    qb = info[i]; qs = i % 4
                    for hd, (Vx, slV, bank) in enumerate(((VA, sl_VA, 5), (VB, sl_VB, 6))):
                        for ch in range(2):
                            g = 2 * ch + hd
                            mm(pbank[bank][:, qs * 128:(qs + 1) * 128], Vx[:, vidx[(r, qt + ch)], :], PT[qb][:, g * 128:(g + 1) * 128], ch == 0, ch == 1,
                               [slV, sl_PT[qb]], [pslot[bank]])
                    if qs == 3 or i == ntile - 1:
                        g0 = i - qs; grp = qtiles[g0:i + 1]; n = len(grp)
                        if d == 1:
                            qt0 = grp[0][1]
                            av = accA[:, qt0 * 128:(qt0 + n) * 128]; bv = accB[:, qt0 * 128:(qt0 + n) * 128]
                            pa = pbank[5][:, 0:n * 128]; pb_ = pbank[6][:, 0:n * 128]
                        else:
                            qt_ = grp[0][1]; r0 = grp[0][0]
                            assert all(g_[1] == qt_ for g_ in grp) and n == 4
                            av = accA[:, qt_ * 128 * d:(qt_ + 1) * 128 * d].rearrange("p (i r) -> p r i", r=d)[:, r0:r0 + 4, :]
                            bv = accB[:, qt_ * 128 * d:(qt_ + 1) * 128 * d].rearrange("p (i r) -> p r i", r=d)[:, r0:r0 + 4, :]
                            pa = pbank[5][:, :].rearrange("p (a b) -> p a b", b=128); pb_ = pbank[6][:, :].rearrange("p (a b) -> p a b", b=128)
                        if pi == 0:
                            cp('act', av, pa, [pslot[5]], [sl_aA]); cp('dve', bv, pb_, [pslot[6]], [sl_aB])
                        else:
                            tt('dve', av, av, pa, ALU.add, [pslot[5], sl_aA], [sl_aA])
                            tt('dve', bv, bv, pb_, ALU.add, [pslot[6], sl_aB], [sl_aB])

                LA = 2
                for i in range(ntile + LA):
                    if i < ntile:
                        emit_qk(i)
                    if i >= LA and OPT.get('att_stop', 99) > 4:
                        emit_pv(i - LA)
            if OPT.get('att_stop', 99) <= 5: continue
            P.dma('sp', Dn[0:64, :], accA[64:128, :], [sl_aA], [sl_VB])
            P.dma('sp', Dn[64:128, :], accB[0:64, :], [sl_aB], [sl_VB])
            P.op('dve', lambda e: e.reciprocal(out=Dn, in_=Dn), [sl_VB], [sl_VB])
            tt('dve', Dn[0:64, :], accA[0:64, :], Dn[0:64, :], ALU.mult, [sl_aA, sl_VB], [sl_VB])
            tt('dve', Dn[64:128, :], accB[64:128, :], Dn[64:128, :], ALU.mult, [sl_aB, sl_VB], [sl_VB])
            cp('act', yb, Dn, [sl_VB], [sl_VA])
            P.dma('sp', yT[512 + pair * 128:512 + (pair + 1) * 128, half * QH:(half + 1) * QH], yb, [sl_VA], ())
            for j in range(QH // 512):
                act(sqb, Dn[:, j * 512:(j + 1) * 512], AF.Square, [sl_VB], [sl_sqb])
                mm(pbank[3][:, :], ones[:], sqb, True, True, [sl_sqb, sl_const], [pslot[3]])
                cp('act', bst, pbank[3][:, :], [pslot[3]], [sl_bst])
                P.dma('sp', ssqd[2 * pair + half:2 * pair + half + 1, j * 512:(j + 1) * 512], bst[0:1, :], [sl_bst], ())
            if debug and pair == 0 and half == 0:
                P.dma('sp', dout("d_yatt0", [128, QH]), Dn, [sl_VB], ())
        P.barrier()

    if 'att' in stages:
        for pair in range(OPT.get('att_pairs', 4)):
            att_pair(pair)

    if 'p2' in stages:
        P.barrier()
        T2 = 512; NH = 11
        hnb = hn[:].rearrange("p a b -> p (a b)")
        off = [0]

        def hget(shape, dt):
            esz = 4 if dt == F32 else 2
            n = int(np.prod(shape[1:])) * esz
            a = off[0] // 2; b = (off[0] + n) // 2; off[0] += n
            assert off[0] <= 98304, off[0]
            v = hnb[:, a:b]
            if dt == F32: v = v.bitcast(F32)
            if len(shape) == 3: v = v.rearrange("p (a b) -> p a b", b=shape[2])
            return v
        off[0] = 16 * DFF * 2
        h2 = hget([128, 8, T2], BF16)
        cv = Carver()
        Wd = cv.get([128, NFF, D], BF16)
        Wo = cv.get([128, 8, D], BF16)
        xres = cv.get([128, 8, T2], F32)
        wpflat = WP[:].rearrange("p a b -> p (a b)")
        actb = wpflat.rearrange("p (a b) -> p a b", b=T2)
        ytile = wpflat[:, 0:8 * T2].rearrange("p (a b) -> p a b", b=T2)
        rs_t = sb("rs_t", [128, T2]); rowt = sb("rowt", [4, T2]); sgl = sb("sgl", [128, 2, T2], BF16)
        onesf = sb("onesf", [4, 128])
        sl_Wd = Slot(); sl_Wo = Slot()
        sl_x = Slot(); sl_h2 = Slot(); sl_rs = Slot(); sl_row = Slot(); sl_sgl = [Slot(), Slot()]
        sl_wpb = [Slot() for _ in range(NH)]
        memset('dve', onesf[:], 1.0, [sl_const])
        if OPT.get('att_pairs', 4) == 0 or 'att' not in stages:
            prefetch_gate_weights()
        wout_v = w_out.rearrange("(kc p) n -> p kc n", p=128)
        wd_v = w_down.rearrange("(kc p) n -> p kc n", p=128)
        ones8 = v_tmp[:, 0:8]
        memset('dve', ones8, 1.0, [sl_small])
        cp('dve', v_tmp[:, 4:8], v_anw, [sl_small], [sl_small])
        for kc in range(8):
            load_cast(wout_v[:, kc, :], D, Wo[:, kc, :], ones8[:, kc:kc + 1], sl_Wo, ['act', 'dve'])
        memset('dve', v_tmp[:, 8:9], 1.0, [sl_small])
        bgu = sb("bgu", [128, 2 * NFF])
        sl_bgu = Slot()
        for wi, (Wt, slw) in enumerate(((Wg, sl_Wg), (Wu, sl_Wu))):
            for j in range(NFF):
                for kc in range(8):
                    mm(pbank[7][:, 128 + wi * NFF + j:129 + wi * NFF + j], Wt[:, kc, j * 128:(j + 1) * 128], sg2b[:, kc:kc + 1], kc == 0, kc == 7,
                       [slw, sl_small], [pslot[7]])
        cp('dve', bgu[:], pbank[7][:, 128:128 + 2 * NFF], [pslot[7]], [sl_bgu])
        xw_o = xw.rearrange("(c p) t -> p c t", p=128)
        yT_v = yT.rearrange("(c p) t -> p c t", p=128)
        outT_v = outT.rearrange("(c p) t -> p c t", p=128)
        sl_xc = [Slot() for _ in range(8)]; sl_h2a = Slot(); sl_h2b = Slot()
        sl_hh = [sl_h2a, sl_h2a, sl_h2a, sl_h2a, sl_h2b, sl_h2b, sl_h2b, sl_h2b]
        rs_a = sgl[:].rearrange("p a b -> p (a b)").bitcast(F32)
        NT = OWN // T2

        def load_rows(tile):
            o0 = tile * T2; half = o0 // QH; oh = o0 % QH
            P.dma('sp', rowt[:], ssqd[half:8:2, oh:oh + T2], (), [sl_row])

        def load_x(tile, oc):
            o0 = tile * T2
            P.dma('pool', xres[:, oc, :], xw_o[:, oc, HALO + o0:HALO + o0 + T2], (), [sl_xc[oc]])

        def prologue(tile):
            o0 = tile * T2
            P.dma('sp', ytile, yT_v[:, :, o0:o0 + T2], (), sl_wpb[0:8] + (sl_stg[0:2] if tile == 0 else []))
            mm(pbank[7][:, :], onesf[:], rowt[:], True, True, [sl_row, sl_const], [pslot[7]])
            act(rs_a, pbank[7][:, :], AF.Ln, [pslot[7], sl_small], sl_sgl, scale=1.0 / 512, bias=v_eps)
            act(rs_a, rs_a, AF.Exp, sl_sgl, sl_sgl, scale=-0.5)
            tt('dve', ytile[:, 4:8, :], ytile[:, 4:8, :], rs_a.unsqueeze(1).to_broadcast([128, 4, T2]), ALU.mult, sl_wpb[0:8] + sl_sgl, sl_wpb[0:8])

        load_rows(0)
        for oc in range(8):
            load_x(0, oc)
        prologue(0)
        h2f = h2[:].rearrange("p a b -> p (a b)").bitcast(F32) if hasattr(h2, 'rearrange') else None
        wstf = wst[:].rearrange("p a b c -> p (a b c)")
        ring_d = [(h2f[:, 0:1024], Slot(), [sl_h2a]), (h2f[:, 1024:2048], Slot(), [sl_h2b]),
                  (wstf[:, 0:1024], Slot(), [sl_wst[0], sl_wst[1], sl_stg[2]]), (wstf[:, 1024:2048], Slot(), [sl_wst[0], sl_wst[1], sl_stg[2]])]
        for j in range(NFF):
            load_cast(wd_v[:, j, :], D, Wd[:, j, :], v_tmp[:, 8:9], sl_Wd, ['act', 'dve'], ring=ring_d)
        for tile in range(NT):
            o0 = tile * T2
            for oc in range(8):
                pb_ = oc % 6
                for kc in range(8):
                    mm(pbank[pb_][:, :], Wo[:, kc, oc * 128:(oc + 1) * 128], ytile[:, kc, :], kc == 0, kc == 7, [sl_Wo] + sl_wpb[0:8], [pslot[pb_]])
                stt('dve', xres[:, oc, :], pbank[pb_][:, :], m_g1[:, oc:oc + 1], xres[:, oc, :], ALU.mult, ALU.add, [pslot[pb_], sl_small, sl_xc[oc]], [sl_xc[oc]])
            act(h2[:, 0:4, :], xres[:, 0:4, :], AF.Square, sl_xc[0:4], [sl_h2a])
            tt('dve', h2[:, 4:8, :], xres[:, 4:8, :], xres[:, 4:8, :], ALU.mult, sl_xc[4:8], [sl_h2b])
            for c in range(8):
                mm(pbank[6][:, :], ones[:], h2[:, c, :], c == 0, c == 7, [sl_hh[c], sl_const], [pslot[6]])
            act(rs_t[:], pbank[6][:, :], AF.Ln, [pslot[6], sl_small], [sl_rs], scale=1.0 / D, bias=v_eps)
            act(rs_t[:], rs_t[:], AF.Exp, [sl_rs], [sl_rs], scale=-0.5)
            tt('dve', h2[:, 0:4, :], xres[:, 0:4, :], rs_t[:].unsqueeze(1).to_broadcast([128, 4, T2]), ALU.mult, sl_xc + [sl_rs], [sl_h2a])
            tt('dve', h2[:, 4:8, :], xres[:, 4:8, :], rs_t[:].unsqueeze(1).to_broadcast([128, 4, T2]), ALU.mult, sl_xc + [sl_rs], [sl_h2b])
            if tile + 1 < NT:
                load_rows(tile + 1)
            for hh_ in range(2):
                for jj in range(NH):
                    j = hh_ * NH + jj
                    pg = jj % 2; pu = 2 + jj % 2
                    for kc in range(8):
                        mm(pbank[pg][:, :], Wg[:, kc, j * 128:(j + 1) * 128], h2[:, kc, :], kc == 0, kc == 7, [sl_Wg, sl_h2a if kc < 4 else sl_h2b], [pslot[pg]])
                    for kc in range(8):
                        mm(pbank[pu][:, :], Wu[:, kc, j * 128:(j + 1) * 128], h2[:, kc, :], kc == 0, kc == 7, [sl_Wu, sl_h2a if kc < 4 else sl_h2b], [pslot[pu]])
                    act(sgl[:, jj % 2, :], pbank[pg][:, :], AF.Silu, [pslot[pg], sl_bgu], [sl_sgl[jj % 2]], bias=bgu[:, j:j + 1])
                    stt('dve', actb[:, jj, :], pbank[pu][:, :], bgu[:, NFF + j:NFF + j + 1], sgl[:, jj % 2, :], ALU.add, ALU.mult,
                        [pslot[pu], sl_bgu, sl_sgl[jj % 2]], [sl_wpb[jj]])
                for oc in range(8):
                    pd_ = 4 + oc % 2
                    for jj in range(NH):
                        j = hh_ * NH + jj
                        mm(pbank[pd_][:, :], Wd[:, j, oc * 128:(oc + 1) * 128], actb[:, jj, :], jj == 0, jj == NH - 1, [sl_Wd, sl_wpb[jj]], [pslot[pd_]])
                    stt('dve', xres[:, oc, :], pbank[pd_][:, :], m_g2[:, oc:oc + 1], xres[:, oc, :], ALU.mult, ALU.add, [pslot[pd_], sl_small, sl_xc[oc]], [sl_xc[oc]])
            if tile + 1 < NT:
                prologue(tile + 1)
            act(h2[:, 0:4, :], xres[:, 0:4, :], AF.Square, sl_xc[0:4], [sl_h2a])
            tt('dve', h2[:, 4:8, :], xres[:, 4:8, :], xres[:, 4:8, :], ALU.mult, sl_xc[4:8], [sl_h2b])
            for c in range(8):
                mm(pbank[6][:, :], ones[:], h2[:, c, :], c == 0, c == 7, [sl_hh[c], sl_const], [pslot[6]])
            act(rs_t[:], pbank[6][:, :], AF.Ln, [pslot[6], sl_small], [sl_rs], scale=1.0 / D, bias=v_eps)
            act(rs_t[:], rs_t[:], AF.Exp, [sl_rs], [sl_rs], scale=-0.5)
            for oc in range(8):
                stt('dve', xres[:, oc, :], xres[:, oc, :], v_fnw[:, oc:oc + 1], rs_t[:], ALU.mult, ALU.mult, [sl_xc[oc], sl_small, sl_rs], [sl_xc[oc]])
                P.dma('sp', outT_v[:, oc, o0:o0 + T2], xres[:, oc, :], [sl_xc[oc]], ())
                if tile + 1 < NT:
                    load_x(tile + 1, oc)
    if debug:
        P.barrier()
        P.dma('sp', dout("d_yT", [D, OWN], BF16), yT, (), ())
    P.barrier()
    P.finish()
    for cm in reversed(ctxs):
        cm.__exit__(None, None, None)
    return nc, list(dbg.keys()), P


def _core_inputs(inputs, consts, core):
    b = core // 4; tq = core % 4; s0 = tq * OWN
    x = inputs['x'][b]
    lo = s0 - HALO
    idx = np.arange(lo, lo + WL)
    valid = (idx >= 0) & (idx < S)
    xwin = np.zeros((WL, D), np.float32)
    xwin[valid] = x[idx[valid]]
    m = {}
    m['xw'] = np.ascontiguousarray(xwin.T)
    m['cT'] = np.ascontiguousarray(inputs['c'][b].reshape(8, 128).T)
    vf = valid.astype(np.float32)
    m['vvalid'] = np.ascontiguousarray(vf.reshape(48, 128).T)
    kv = np.zeros((128, NKV), np.float32)
    for (half, d, r, kc), (w0, ci) in ATT_TILES.items():
        kv[:, ci] = vf[w0 + np.arange(128) * d]
    m['kvalid'] = kv
    m.update(consts)
    return m


def _shared_inputs(inputs):
    f = lambda a: np.ascontiguousarray(np.asarray(a, np.float32))
    m = {}
    m['w_ada'] = f(inputs['w_ada'][0]); m['b_adaT'] = f(inputs['b_ada'][0].reshape(48, 128).T)
    m['n1w'] = f(inputs['norm1_w'][0].reshape(8, 128).T); m['w_in'] = f(inputs['w_in'][0])
    lbr = np.asarray(inputs['hg_lower_bound'], np.float32)
    m['lbA'] = f(lbr.reshape(2, 2, 4, 128).transpose(3, 2, 1, 0).reshape(128, 16))
    m['gnw'] = f(inputs['hg_norm_w'][0].reshape(4, 128).T); m['anw'] = f(inputs['attn_norm_w'][0].reshape(4, 128).T)
    m['w_out'] = f(inputs['w_out'][0]); m['n2w'] = f(inputs['norm2_w'][0].reshape(8, 128).T)
    m['w_gate'] = f(inputs['w_gate'][0]); m['w_up'] = f(inputs['w_up'][0]); m['w_down'] = f(inputs['w_down'][0])
    m['fnw'] = f(inputs['final_norm_w'].reshape(8, 128).T)
    m['biasT'] = _bias_tables(np.asarray(inputs['rel_bias'], np.float32))
    return m


_CACHE = {}


def kernel(**inputs):
    inputs = {k: np.asarray(v) for k, v in inputs.items()}
    if 'nc' not in _CACHE:
        _CACHE['nc'] = build_program()[0]
    nc = _CACHE['nc']
    consts = _host_consts()
    shared = _shared_inputs(inputs)
    in_maps = []
    for core in range(NCORE):
        m = _core_inputs(inputs, consts, core)
        m.update(shared)
        in_maps.append(m)
    res = run_bass_kernel_spmd(nc, in_maps, core_ids=list(range(NCORE)))
    out = np.zeros((2, S, D), np.float32)
    for core in range(NCORE):
        b = core // 4; s0 = (core % 4) * OWN
        out[b, s0:s0 + OWN] = np.asarray(res.results[core]['outT']).T
    return out
```
